# Optimizing a Trainium2 kernel written in Bass

```python
import jax, jax.numpy as jnp
from jax import lax
import numpy as np

D_MODEL = 2048
BATCH = 4
SEQ = 2048
DEPTH = 4

CHUNK = 64
N_HEADS = 16
HEAD_DIM = D_MODEL // N_HEADS
D_FF = 5632
BLOCK_Q = 128
LEFT_CHUNKS = 8
BAND = LEFT_CHUNKS + 1
REL_CLIP = 256
N_A = DEPTH // 2
N_B = DEPTH - N_A
EPS = 1e-6

kernel_name = "yoco_stickbreak_chunkband_macaron"


def rms_norm(x, g):
    x32 = x.astype(jnp.float32)
    y = x32 * lax.rsqrt(jnp.mean(x32 * x32, axis=-1, keepdims=True) + EPS)
    return (y * g.astype(jnp.float32)).astype(x.dtype)


def swiglu(x, w_gate, w_up, w_down):
    return (jax.nn.silu(x @ w_gate) * (x @ w_up)) @ w_down


def stick_breaking_attention(q, k, v):
    seq = q.shape[1]
    scale = HEAD_DIM ** -0.5
    outs = []
    for i in range(seq // BLOCK_Q):
        q0 = i * BLOCK_Q
        kl = q0 + BLOCK_Q
        z = jnp.einsum('bqhd,bkhd->bhqk', q[:, q0:kl], k[:, :kl]).astype(jnp.float32) * scale
        t_pos = q0 + jnp.arange(BLOCK_Q)
        s_pos = jnp.arange(kl)
        mask = s_pos[None, :] < t_pos[:, None]
        log_1m = jnp.where(mask, jax.nn.log_sigmoid(-z), 0.0)
        log_a = jax.nn.log_sigmoid(z) + lax.cumsum(log_1m, axis=3, reverse=True) - log_1m
        a = jnp.where(mask, jnp.exp(log_a), 0.0)
        outs.append(jnp.einsum('bhqk,bkhd->bqhd', a.astype(v.dtype), v[:, :kl]))
    out = jnp.concatenate(outs, axis=1)
    return out.reshape(out.shape[0], seq, N_HEADS * HEAD_DIM)


def chunked_band_attention(q, k_pad, v_pad, rel_bias):
    b, seq, h, dh = q.shape
    nc = seq // CHUNK
    scale = HEAD_DIM ** -0.5
    qc = q.reshape(b, nc, CHUNK, h, dh)
    kc = k_pad.reshape(b, nc + LEFT_CHUNKS, CHUNK, h, dh)
    vc = v_pad.reshape(b, nc + LEFT_CHUNKS, CHUNK, h, dh)
    scores = jnp.concatenate(
        [jnp.einsum('bnqhd,bnkhd->bhnqk', qc, kc[:, j:j + nc]) for j in range(BAND)],
        axis=-1).astype(jnp.float32) * scale
    i_pos = jnp.arange(CHUNK)
    p_pos = jnp.arange(BAND * CHUNK)
    rel = LEFT_CHUNKS * CHUNK + i_pos[:, None] - p_pos[None, :]
    idx = jnp.clip(rel, -REL_CLIP, REL_CLIP) + REL_CLIP
    bias = rel_bias[:, idx].astype(jnp.float32)
    c_idx = jnp.arange(nc)
    valid = (c_idx[:, None] - LEFT_CHUNKS + p_pos[None, :] // CHUNK) >= 0
    scores = scores + bias[None, :, None]
    scores = jnp.where(valid[None, None, :, None, :], scores, -jnp.inf)
    probs = jax.nn.softmax(scores, axis=-1).astype(v_pad.dtype)
    out = jnp.einsum('bhnqk,bnkhd->bnqhd', probs[..., :CHUNK], vc[:, 0:nc])
    for j in range(1, BAND):
        out = out + jnp.einsum('bhnqk,bnkhd->bnqhd',
                               probs[..., j * CHUNK:(j + 1) * CHUNK], vc[:, j:j + nc])
    return out.reshape(b, seq, h * dh)


def setup_inputs(seed: int = 0) -> dict:
    key = jax.random.key(seed)
    ks = jax.random.split(key, 16)
    f32 = jnp.float32
    d, f = D_MODEL, D_FF
    x = jax.random.normal(ks[0], (BATCH, SEQ, d), f32)
    g_ffn = 1.0 + 0.05 * jax.random.normal(ks[1], (DEPTH, 2, d), f32)
    w_ffn_gate = jax.random.normal(ks[2], (DEPTH, 2, d, f), f32) * d ** -0.5
    w_ffn_up = jax.random.normal(ks[3], (DEPTH, 2, d, f), f32) * d ** -0.5
    w_ffn_down = jax.random.normal(ks[4], (DEPTH, 2, f, d), f32) * f ** -0.5
    g_mix = 1.0 + 0.05 * jax.random.normal(ks[5], (DEPTH, d), f32)
    w_qkv_a = jax.random.normal(ks[6], (N_A, d, 3 * d), f32) * d ** -0.5
    w_o_a = jax.random.normal(ks[7], (N_A, d, d), f32) * d ** -0.5
    g_kv = 1.0 + 0.05 * jax.random.normal(ks[8], (d,), f32)
    w_kv_shared = jax.random.normal(ks[9], (d, 2 * d), f32) * d ** -0.5
    w_q_b = jax.random.normal(ks[10], (N_B, d, d), f32) * d ** -0.5
    w_o_b = jax.random.normal(ks[11], (N_B, d, d), f32) * d ** -0.5
    rel_bias_b = 0.1 * jax.random.normal(ks[12], (N_B, N_HEADS, 2 * REL_CLIP + 1), f32)
    g_final = 1.0 + 0.05 * jax.random.normal(ks[13], (d,), f32)
    return {"x": x, "g_ffn": g_ffn, "w_ffn_gate": w_ffn_gate, "w_ffn_up": w_ffn_up,
            "w_ffn_down": w_ffn_down, "g_mix": g_mix, "w_qkv_a": w_qkv_a, "w_o_a": w_o_a,
            "g_kv": g_kv, "w_kv_shared": w_kv_shared, "w_q_b": w_q_b, "w_o_b": w_o_b,
            "rel_bias_b": rel_bias_b, "g_final": g_final}


def reference(x, g_ffn, w_ffn_gate, w_ffn_up, w_ffn_down, g_mix, w_qkv_a, w_o_a,
              g_kv, w_kv_shared, w_q_b, w_o_b, rel_bias_b, g_final):
    b, seq, d = x.shape
    h = x
    k_pad = None
    v_pad = None
    for layer in range(DEPTH):
        h = h + 0.5 * swiglu(rms_norm(h, g_ffn[layer, 0]), w_ffn_gate[layer, 0],
                             w_ffn_up[layer, 0], w_ffn_down[layer, 0])
        if layer < N_A:
            u = rms_norm(h, g_mix[layer])
            qkv = (u @ w_qkv_a[layer]).reshape(b, seq, 3, N_HEADS, HEAD_DIM)
            mix = stick_breaking_attention(qkv[:, :, 0], qkv[:, :, 1], qkv[:, :, 2])
            h = h + mix @ w_o_a[layer]
        else:
            if layer == N_A:
                kv = (rms_norm(h, g_kv) @ w_kv_shared).reshape(b, seq, 2, N_HEADS, HEAD_DIM)
                pad = ((0, 0), (LEFT_CHUNKS * CHUNK, 0), (0, 0), (0, 0))
                k_pad = jnp.pad(kv[:, :, 0], pad)
                v_pad = jnp.pad(kv[:, :, 1], pad)
            lb = layer - N_A
            u = rms_norm(h, g_mix[layer])
            q = (u @ w_q_b[lb]).reshape(b, seq, N_HEADS, HEAD_DIM)
            mix = chunked_band_attention(q, k_pad, v_pad, rel_bias_b[lb])
            h = h + mix @ w_o_b[lb]
        h = h + 0.5 * swiglu(rms_norm(h, g_ffn[layer, 1]), w_ffn_gate[layer, 1],
                             w_ffn_up[layer, 1], w_ffn_down[layer, 1])
    return rms_norm(h, g_final)
```

```python
import numpy as np
from contextlib import ExitStack
import concourse.bass as bass
import concourse.mybir as mybir
from concourse.bass_utils import run_bass_kernel_spmd

F32 = mybir.dt.float32
BF16 = mybir.dt.bfloat16
AF = mybir.ActivationFunctionType
ALU = mybir.AluOpType

EPS = 1e-6
CHUNK = 64
LEFT_CHUNKS = 8
REL_CLIP = 256
NEG = -1.0e30
PAIRS = [[0, 1], [2, 3], [4, 5], [6, 7]]


class Cfg:
    def __init__(self, H=16, DFF=5632, SEQ=2048, DEPTH=4, BATCH=4):
        self.H = H
        self.D = 128 * H
        self.KC = H
        self.DFF = DFF
        self.FC = DFF // 128
        self.NG = self.FC // 2
        self.SEQ = SEQ
        self.OWN = SEQ // 2
        self.NBL = self.OWN // 128
        self.NB = SEQ // 128
        self.TT = min(512, self.OWN)
        self.NTT = self.OWN // self.TT
        self.DEPTH = DEPTH
        self.NA = DEPTH // 2
        self.NBD = DEPTH - self.NA
        self.BATCH = BATCH
        self.NV = 3 * DEPTH + 2


def own(gb):
    return ((gb + 1) // 2) % 2


def gblock(r, lb):
    return 2 * lb + ((lb + r) % 2)


class Counter:
    LIMIT = 16000

    def __init__(self, prog, step):
        self.prog = prog
        self.step = step
        self.h = prog.new_sem()
        self.v = 0

    def bump(self):
        if self.v + self.step > self.LIMIT:
            self.h = self.prog.new_sem()
            self.v = 0
        self.v += self.step
        return (self.h, self.v, self.step)


class Prog:
    ENG = ("pe", "act", "dve", "pool", "sp")

    def __init__(self, nc, es):
        self.nc = nc
        self.es = es
        self.streams = {e: [] for e in self.ENG}
        self.nsem = 0

    def new_sem(self):
        self.nsem += 1
        return self.es.enter_context(self.nc.semaphore(f"sm{self.nsem}"))

    def op(self, eng, fn, waits=(), cnt=None):
        tok = cnt.bump() if cnt is not None else None
        ws = []

        def flat(w):
            if w is None:
                return
            if isinstance(w, list):
                for x in w:
                    flat(x)
            else:
                ws.append(w)

        flat(list(waits))
        self.streams[eng].append((ws, fn, tok))
        return tok

    def emit(self, name, eng):
        seen = {}
        for ws, fn, tok in self.streams[name]:
            for (h, v, _s) in ws:
                k = id(h)
                if seen.get(k, 0) < v:
                    eng.wait_ge(h, v)
                    seen[k] = v
            ins = fn(eng)
            if tok is not None:
                ins.then_inc(tok[0], tok[2])


class Ring:
    def __init__(self, slots, prog=None):
        self.slots = slots
        self.free = [None] * len(slots)
        self.i = 0
        self.cnt = [Counter(prog, 16) for _ in slots] if prog is not None else None

    def take(self):
        s = self.i % len(self.slots)
        self.i += 1
        return s


class Builder:
    def __init__(self, cfg):
        self.c = cfg

    def build(self):
        c = self.c
        nc = bass.Bass("TRN2", target_bir_lowering=False)
        self.nc = nc
        D, DFF, OWN, KC, H = c.D, c.DFF, c.OWN, c.KC, c.H

        def din(name, shape, dt=F32):
            return nc.dram_tensor(name, list(shape), dt, kind="ExternalInput").ap()

        self.xT = din("xT", [D, OWN])
        self.gv_d = din("gv", [128, c.NV * KC])
        self.cst_d = din("cst", [128, 512])
        self.m2_d = din("m2", [128, 2 * 256])
        self.mb_d = din("mb", [128, 2 * 256])
        self.bt_d = din("bt", [c.NBD, H, 128, 2 * 6 * 128])
        self.w_gate = din("w_ffn_gate", [c.DEPTH, 2, D, DFF])
        self.w_up = din("w_ffn_up", [c.DEPTH, 2, D, DFF])
        self.w_down = din("w_ffn_down", [c.DEPTH, 2, DFF, D])
        self.w_qkv = din("w_qkv_a", [c.NA, D, 3 * D])
        self.w_oa = din("w_o_a", [c.NA, D, D])
        self.w_kv = din("w_kv_shared", [D, 2 * D])
        self.w_qb = din("w_q_b", [c.NBD, D, D])
        self.w_ob = din("w_o_b", [c.NBD, D, D])
        self.outT = nc.dram_tensor("outT", [D, OWN], F32, kind="ExternalOutput").ap()
        self.dbg_on = getattr(self, "dbg_on", False)
        if self.dbg_on:
            self.dbg_d = nc.dram_tensor("dbg", [128, 8192], F32, kind="ExternalOutput").ap()
            self.dbg_col = 0
            self.dbg_map = {}
            self.dbg_toks = []

        self.NX = c.NA + 1
        self.kloc = [nc.dram_tensor(f"kloc{i}", [D, OWN], BF16, kind="Internal").ap() for i in range(self.NX)]
        self.vloc = [nc.dram_tensor(f"vloc{i}", [OWN, D], BF16, kind="Internal").ap() for i in range(self.NX)]
        self.qloc = nc.dram_tensor("qloc", [D, OWN], BF16, kind="Internal").ap()
        self.rows_k = max(128, min(D, (2 << 20) // (OWN * 2)))
        self.rows_v = max(128, min(OWN, (2 << 20) // (D * 2)))
        self.kall = [[nc.dram_tensor(f"kall{i}_{p}", [2 * self.rows_k, OWN], BF16, kind="Internal").ap()
                      for p in range(D // self.rows_k)] for i in range(self.NX)]
        self.vall = [[nc.dram_tensor(f"vall{i}_{p}", [2 * self.rows_v, D], BF16, kind="Internal").ap()
                      for p in range(OWN // self.rows_v)] for i in range(self.NX)]

        with ExitStack() as es:
            self.es = es
            self.P = Prog(nc, es)
            self.alloc()
            self.program()
            block = es.enter_context(nc.Block())
            P = self.P

            @block.tensor
            def _(e):
                P.emit("pe", e)

            @block.scalar
            def _(e):
                P.emit("act", e)

            @block.vector
            def _(e):
                P.emit("dve", e)

            @block.gpsimd
            def _(e):
                P.emit("pool", e)

            @block.sync
            def _(e):
                P.emit("sp", e)
        return nc

    def alloc(self):
        c, nc, P = self.c, self.nc, self.P
        KC, OWN, TT, H, SEQ = c.KC, c.OWN, c.TT, c.H, c.SEQ
        self.off = 16640

        def sb(name, shape, dt, at=None):
            size = int(np.prod(shape[1:])) * (4 if dt == F32 else 2)
            if at is None:
                at = self.off
                self.off = (at + size + 63) // 64 * 64
            t = nc.alloc_sbuf_tensor_at(name, list(shape), dt, offset=at)
            return t, at + size

        self.ones_f, _ = sb("ones_f", [128, 128], F32)
        self.ident_f, _ = sb("ident_f", [128, 128], F32)
        self.ones_b, _ = sb("ones_b", [128, 128], BF16)
        self.ident_b, _ = sb("ident_b", [128, 128], BF16)
        self.m2_f, _ = sb("m2_f", [128, 2, 256], F32)
        self.m2_b, _ = sb("m2_b", [128, 2, 256], BF16)
        self.mb_b, _ = sb("mb_b", [128, 2, 256], BF16)
        self.gv, _ = sb("gv", [128, c.NV * KC], F32)
        self.nT, _ = sb("nT", [128, 4], F32)
        self.one1, _ = sb("one1", [128, 4], F32)
        self.hT, _ = sb("hT", [128, KC, OWN], F32)
        self.xn, _ = sb("xn", [128, KC, OWN], BF16)
        R0 = self.off
        self.R0 = R0
        o = R0
        self.rstd, o = sb("rstd", [128, OWN], F32, at=o)
        self.sq = []
        self.sqb = []
        for i in range(2):
            tb_, _ = sb(f"sqb{i}", [128, OWN], BF16, at=o)
            self.sqb.append(tb_)
            t, o = sb(f"sq{i}", [128, OWN], F32, at=o)
            self.sq.append(t)
        self.actb = []
        act_at = o
        for i in range(4):
            t, o = sb(f"act{i}", [128, 2, OWN], BF16, at=o)
            self.actb.append(t)
        self.sg = []
        sg_at = o
        for i in range(4):
            t, o = sb(f"sg{i}", [128, TT], F32, at=o)
            self.sg.append(t)
        wg, wu, wd = [], [], []
        for i in range(2):
            t, o = sb(f"wg{i}", [128, KC, 256], BF16, at=o)
            wg.append(t)
        for i in range(2):
            t, o = sb(f"wu{i}", [128, KC, 256], BF16, at=o)
            wu.append(t)
        for i in range(4):
            t, o = sb(f"wd{i}", [128, 2, c.D], BF16, at=o)
            wd.append(t)
        self.wg, self.wu, self.wd = Ring(wg, P), Ring(wu, P), Ring(wd, P)
        self.wp = Ring(wg + wu, P)
        ffn_end = o
        self.kst = [nc.alloc_sbuf_tensor_at(f"kst{i}", [128, OWN], BF16, offset=sg_at + i * OWN * 2)
                    for i in range(2)]
        assert 2 * OWN * 2 <= 4 * TT * 4
        self.vst = [nc.alloc_sbuf_tensor_at(f"vst{i}", [128, c.NBL, 256], BF16, offset=act_at + i * 2 * OWN * 2)
                    for i in range(2)]
        assert c.NBL * 256 * 2 <= 2 * OWN * 2
        o = R0
        self.kh, self.vh, self.qh = [], [], []
        for i in range(2):
            t, o = sb(f"kh{i}", [128, SEQ], BF16, at=o)
            self.kh.append(t)
        for i in range(2):
            t, o = sb(f"vh{i}", [128, c.NB, 128], BF16, at=o)
            self.vh.append(t)
        for i in range(2):
            t, o = sb(f"qh{i}", [128, OWN], BF16, at=o)
            self.qh.append(t)
        o_common = o
        self.E, self.SP, self.C, self.A, self.AT = [], [], [], [], []
        for i in range(3):
            t, o = sb(f"E{i}", [128, SEQ], F32, at=o)
            self.E.append(t)
        for i in range(2):
            t, o = sb(f"SP{i}", [128, SEQ + 16], F32, at=o)
            self.SP.append(t)
        for i in range(2):
            t, o = sb(f"C{i}", [128, SEQ], F32, at=o)
            self.C.append(t)
        for i in range(3):
            t, o = sb(f"A{i}", [128, SEQ], BF16, at=o)
            self.A.append(t)
        for i in range(2):
            t, o = sb(f"AT{i}", [128, SEQ], BF16, at=o)
            self.AT.append(t)
        sb_end = o
        o = o_common
        self.bt = []
        for i in range(2):
            t, o = sb(f"bt{i}", [128, 2, 768], F32, at=o)
            self.bt.append(t)
        self.S = []
        for i in range(2):
            t, o = sb(f"S{i}", [128, 768], F32, at=o)
            self.S.append(t)
        self.Pb = []
        for i in range(2):
            t, o = sb(f"Pb{i}", [128, 768], BF16, at=o)
            self.Pb.append(t)
        self.rden = []
        for i in range(2):
            t, o = sb(f"rden{i}", [128, 128], F32, at=o)
            self.rden.append(t)
        band_end = o
        self.sb_total = max(sb_end, band_end, ffn_end)
        assert self.sb_total <= 229344, self.sb_total
        self.psA = self.es.enter_context(nc.psum_tensor("psA", [128, 2048], F32))
        self.psB = self.es.enter_context(nc.psum_tensor("psB", [128, 2048], F32))
        self.pfree = {}
        self.c_pe = Counter(P, 1)
        self.c_act = Counter(P, 1)
        self.c_dve = Counter(P, 1)
        self.c_w = Counter(P, 16)
        self.c_ld = Counter(P, 16)
        self.c_st = Counter(P, 16)
        self.c_dbg = Counter(P, 16)
        self.c_cc = Counter(P, 1)
        self.c_pool = Counter(P, 1)
        self.c_qh = [Counter(P, 16) for _ in range(2)]
        self.c_kh = [Counter(P, 16) for _ in range(2)]
        self.c_vh = [Counter(P, 16) for _ in range(2)]
        self.c_bt = [Counter(P, 16) for _ in range(2)]
        self.c_kst = [Counter(P, 16) for _ in range(2)]
        self.c_vst = [Counter(P, 16) for _ in range(2)]

    def bankA(self, i, n=None):
        n = self.c.TT if n is None else n
        return self.psA[:, i * 512:i * 512 + n]

    def bankB(self, i, n=None):
        n = self.c.TT if n is None else n
        return self.psB[:, i * 512:i * 512 + n]

    def program(self):
        c, P = self.c, self.P
        KC, OWN = c.KC, c.OWN
        toks = []
        cst = self.cst_d
        toks.append(P.op("sp", lambda e: e.dma_start(out=self.ones_f[:], in_=cst[:, 0:128]), cnt=self.c_ld))
        toks.append(P.op("sp", lambda e: e.dma_start(out=self.ident_f[:], in_=cst[:, 256:384]), cnt=self.c_ld))
        toks.append(P.op("sp", lambda e: e.dma_start(out=self.gv[:], in_=self.gv_d), cnt=self.c_ld))
        toks.append(P.op("sp", lambda e: e.dma_start(out=self.m2_f[:], in_=self.m2_d.rearrange("p (a b) -> p a b", a=2)), cnt=self.c_ld))
        toks.append(P.op("pool", lambda e: e.dma_start(out=self.ones_b[:], in_=cst[:, 0:128]), cnt=self.c_w))
        toks.append(P.op("pool", lambda e: e.dma_start(out=self.ident_b[:], in_=cst[:, 256:384]), cnt=self.c_w))
        toks.append(P.op("pool", lambda e: e.dma_start(out=self.m2_b[:], in_=self.m2_d.rearrange("p (a b) -> p a b", a=2)), cnt=self.c_w))
        toks.append(P.op("pool", lambda e: e.dma_start(out=self.mb_b[:], in_=self.mb_d.rearrange("p (a b) -> p a b", a=2)), cnt=self.c_w))
        t1 = P.op("dve", lambda e: e.memset(self.one1[:], 1.0), cnt=self.c_dve)
        self.const_tok = toks + [t1]
        xv = self.xT.rearrange("(c p) t -> p c t", p=128)
        self.h_tok = []
        self.c_x = [Counter(P, 16) for _ in range(KC)]
        hk = {}
        for g in range(KC):
            q = "sp" if g % 2 == 0 else "act"
            t = P.op(q, lambda e, g=g: e.dma_start(out=self.hT[:, g, :], in_=xv[:, g, :]), cnt=self.c_x[g])
            self.h_tok.append(t)
            hk[g] = [t] + (self.const_tok if g == 0 else [])
        self.h_tok_k = hk
        self.h_tok += self.const_tok
        self.xn_free = []
        self.rstd_free = None
        self.sq_free = [None, None]
        self.attn_done = []
        self.qT_free = []

        nst = getattr(self, "nstages", 99)
        si = 0
        for l in range(c.DEPTH):
            if si >= nst:
                break
            self.ffn(l, 0)
            si += 1
            if si >= nst:
                break
            if l < c.NA:
                self.rmsnorm(c.DEPTH * 2 + l)
                self.x_tok = []
                self.proj_k(self.w_qkv[l][:, c.D:2 * c.D], l)
                self.proj_v(self.w_qkv[l][:, 2 * c.D:3 * c.D], l, after_prefetch=lambda l=l: self.exchange(l, "k"))
                self.proj_q(self.w_qkv[l][:, 0:c.D], after_prefetch=lambda l=l: self.exchange(l, "v"))
                self.attn_sb(l)
                self.oproj(self.w_oa[l])
            else:
                lb = l - c.NA
                if lb == 0:
                    self.rmsnorm(3 * c.DEPTH)
                    self.x_tok = []
                    self.proj_k(self.w_kv[:, 0:c.D], c.NA)
                    self.proj_v(self.w_kv[:, c.D:2 * c.D], c.NA, after_prefetch=lambda: self.exchange(c.NA, "k"))
                    self.pending_xv = True
                self.rmsnorm(c.DEPTH * 2 + l)
                if getattr(self, "pending_xv", False):
                    self.pending_xv = False
                    self.proj_q(self.w_qb[lb], after_prefetch=lambda: self.exchange(c.NA, "v"))
                else:
                    self.proj_q(self.w_qb[lb])
                self.attn_band(lb)
                self.oproj(self.w_ob[lb])
            si += 1
            if si >= nst:
                break
            self.ffn(l, 1)
            si += 1
        if si >= nst and nst < 3 * c.DEPTH + 1:
            self.dump_h()
        else:
            self.final_norm(3 * c.DEPTH + 1)

    def dbg(self, name, ap, n, waits):
        if not self.dbg_on:
            return
        c0 = self.dbg_col
        self.dbg_col += n
        self.dbg_map[name] = (c0, n)
        t = self.P.op("pool", lambda e: e.dma_start(out=self.dbg_d[:, c0:c0 + n], in_=ap), waits=waits, cnt=self.c_dbg)
        self.dbg_toks.append(t)
        return t

    def dump_h(self):
        P = self.P
        ov = self.outT.rearrange("(c p) t -> p c t", p=128)
        t_o = P.op("sp", lambda e: e.dma_start(out=ov, in_=self.hT[:]), waits=[self.h_tok], cnt=self.c_st)
        for name in Prog.ENG:
            P.op(name, lambda e: e.wait_ge(t_o[0], t_o[1]))
        if self.dbg_on:
            for t in self.dbg_toks:
                P.op("sp", lambda e, t=t: e.wait_ge(t[0], t[1]))

    def rings_to_proj(self):
        if getattr(self, "ring_mode", "ffn") == "proj":
            return
        self.ring_mode = "proj"
        self.wp.free = [self.wg.free[0], self.wg.free[1], self.wu.free[0], self.wu.free[1]]

    def rings_to_ffn(self):
        if getattr(self, "ring_mode", "ffn") == "ffn":
            return
        self.ring_mode = "ffn"
        allp = [t for t in self.wp.free]
        self.wg.free = [[self.wg.free[0], allp[0]], [self.wg.free[1], allp[1]]]
        self.wu.free = [[self.wu.free[0], allp[2]], [self.wu.free[1], allp[3]]]

    def load_w(self, ring, src, extra=()):
        P = self.P
        s = ring.take()
        dst = ring.slots[s]
        tok = P.op("pool", lambda e: e.dma_start(out=dst[:], in_=src), waits=[ring.free[s]] + list(extra), cnt=ring.cnt[s])
        return s, tok

    def wtile(self, W, t):
        return W.rearrange("(c p) f -> p c f", p=128)[:, :, t * 256:(t + 1) * 256]

    def rmsnorm(self, gi, to_out=False):
        c, P = self.c, self.P
        KC, TT, NTT, D = c.KC, c.TT, c.NTT, c.D
        hT, xn, rstd = self.hT, self.xn, self.rstd
        hk = getattr(self, "h_tok_k", None)
        t_mm = None
        for k in range(KC):
            sqb = self.sqb[k % 2]
            wk = hk[k] if hk is not None else (self.h_tok if k == 0 else None)
            t_sq = P.op("act", lambda e, sqb=sqb, k=k: e.activation(out=sqb[:], in_=hT[:, k, :], func=AF.Square),
                        waits=[wk, self.sq_free[k % 2]], cnt=self.c_act)
            for tt in range(NTT):
                last = tt == NTT - 1
                w = [t_sq]
                if k == 0:
                    w.append(self.pfree.get(("B", tt)))
                t = P.op("pe", lambda e, sqb=sqb, tt=tt, k=k: e.matmul(
                    self.bankB(tt), self.ones_b[:], sqb[:, tt * TT:(tt + 1) * TT], start=(k == 0), stop=(k == KC - 1)),
                    waits=w, cnt=self.c_pe if last else None)
                if last:
                    t_mm = t
            self.sq_free[k % 2] = t_mm
        t_r = None
        for tt in range(NTT):
            t_r = P.op("act", lambda e, tt=tt: e.activation(out=rstd[:, tt * TT:(tt + 1) * TT], in_=self.bankB(tt),
                                                            func=AF.Ln, scale=1.0 / D, bias=EPS),
                       waits=[t_mm, self.rstd_free], cnt=self.c_act)
            self.pfree[("B", tt)] = t_r
        t_rs = P.op("act", lambda e: e.activation(out=rstd[:], in_=rstd[:], func=AF.Exp, scale=-0.5), cnt=self.c_act)
        KD = KC
        dst = hT if to_out else xn
        toks = {}
        first = {"dve": True, "pool": True}
        for k in range(KC):
            eng = "dve" if k < KD else "pool"
            fn = lambda e, k=k: e.scalar_tensor_tensor(out=dst[:, k, :], in0=hT[:, k, :],
                                                       scalar=self.gv[:, gi * KC + k:gi * KC + k + 1],
                                                       in1=rstd[:], op0=ALU.mult, op1=ALU.mult)
            w = [t_rs, self.h_tok, self.xn_free] if first[eng] else []
            first[eng] = False
            toks[k] = P.op(eng, fn, waits=w, cnt=self.c_dve if eng == "dve" else self.c_pool)
        last_d = toks[KD - 1]
        last_p = toks[KC - 1] if KD < KC else None
        self.rstd_free = [last_d, last_p]
        self.xn_ready = [last_d, last_p]
        self.xn_ready_k = toks
        self.xn_free = []
        self.h_tok_k = None
        return [last_d, last_p]

    def final_norm(self, gi):
        P = self.P
        self.rmsnorm(gi, to_out=True)
        ov = self.outT.rearrange("(c p) t -> p c t", p=128)
        toks = self.xn_ready_k
        outs = []
        for k in range(self.c.KC):
            outs.append(P.op("sp", lambda e, k=k: e.dma_start(out=ov[:, k, :], in_=self.hT[:, k, :]), waits=[toks[k]], cnt=self.c_st))
        t_o = outs[-1]
        for name in Prog.ENG:
            P.op(name, lambda e: e.wait_ge(t_o[0], t_o[1]))
        if self.dbg_on:
            for t in self.dbg_toks:
                P.op("sp", lambda e, t=t: e.wait_ge(t[0], t[1]))

    def ffn(self, l, j):
        c, P = self.c, self.P
        KC, TT, NTT, NG, OWN = c.KC, c.TT, c.NTT, c.NG, c.OWN
        hT, xn = self.hT, self.xn
        self.rmsnorm(l * 2 + j)
        self.rings_to_ffn()
        Wg, Wu, Wd = self.w_gate[l, j], self.w_up[l, j], self.w_down[l, j]
        wdv = Wd.rearrange("(c p) d -> p c d", p=128)
        extra_first = list(self.attn_done)
        loads = {}

        loads_d = {}

        def issue(g):
            ex = extra_first if g < 2 else []
            sg_, tg = self.load_w(self.wg, self.wtile(Wg, g), ex)
            su_, tu = self.load_w(self.wu, self.wtile(Wu, g), ex)
            loads[g] = (sg_, tg, su_, tu)

        def issue_d(g):
            ex = extra_first if g < 2 else []
            loads_d[g] = self.load_w(self.wd, wdv[:, 2 * g:2 * g + 2, :], ex + list(self.qT_free))

        issue(0)
        if NG > 1:
            issue(1)
        for g_ in range(min(4, NG)):
            issue_d(g_)
        act_ready = {}
        act_free = getattr(self, "act_free", [None] * 4)
        sg_free = getattr(self, "sg_free", [None] * 4)
        sgi = 0
        h_new = []
        dn_last = {}

        def down(gs):
            t_pe = None
            nmm = 2 * len(gs)
            for dc in range(KC):
                st = dc % 2
                mi = 0
                for g in gs:
                    sd_, td = loads_d[g]
                    wdt = self.wd.slots[sd_]
                    ab = self.actb[g % 4]
                    for fl in range(2):
                        for tt in range(NTT):
                            bi = st * 2 + tt if NTT == 2 else st
                            w = []
                            if mi == 0:
                                w.append(self.pfree.get(("B", bi)))
                            if dc == 0 and fl == 0 and tt == 0:
                                w += [td] + act_ready[g]
                            last = (mi == nmm - 1 and tt == NTT - 1)
                            t = P.op("pe", lambda e, bi=bi, fl=fl, dc=dc, tt=tt, wdt=wdt, ab=ab, mi=mi: e.matmul(
                                self.bankB(bi), wdt[:, fl, dc * 128:(dc + 1) * 128], ab[:, fl, tt * TT:(tt + 1) * TT],
                                start=(mi == 0), stop=(mi == nmm - 1)), waits=w, cnt=self.c_pe if last else None)
                            if last:
                                t_pe = t
                        mi += 1
                for tt in range(NTT):
                    bi = st * 2 + tt if NTT == 2 else st
                    t_acc = P.op("dve", lambda e, bi=bi, dc=dc, tt=tt: e.scalar_tensor_tensor(
                        out=hT[:, dc, tt * TT:(tt + 1) * TT], in0=self.bankB(bi), scalar=0.5,
                        in1=hT[:, dc, tt * TT:(tt + 1) * TT], op0=ALU.mult, op1=ALU.add),
                        waits=[t_pe], cnt=self.c_dve)
                    self.pfree[("B", bi)] = t_acc
                    dn_last[(dc, tt)] = t_acc
            for g in gs:
                self.wd.free[loads_d[g][0]] = t_pe
                act_free[g % 4] = t_pe
            for g in gs:
                if g + 4 < NG:
                    issue_d(g + 4)

        for g in range(NG):
            sg_, tg, su_, tu = loads[g]
            wgt, wut = self.wg.slots[sg_], self.wu.slots[su_]
            act_ready[g] = []
            for fl in range(2):
                t_g = None
                for k in range(KC):
                    for tt in range(NTT):
                        w = []
                        if k == 0:
                            w.append(self.pfree.get(("A", tt)))
                            if tt == 0 and fl == 0:
                                w.append(tg)
                        if g == 0 and fl == 0 and tt == 0:
                            w.append(self.xn_ready_k[k] if self.xn_ready_k is not None else (self.xn_ready if k == 0 else None))
                        last = (k == KC - 1 and tt == NTT - 1)
                        t = P.op("pe", lambda e, k=k, tt=tt, fl=fl, wgt=wgt: e.matmul(
                            self.bankA(tt), wgt[:, k, fl * 128:(fl + 1) * 128], xn[:, k, tt * TT:(tt + 1) * TT],
                            start=(k == 0), stop=(k == KC - 1)), waits=w, cnt=self.c_pe if last else None)
                        if last:
                            t_g = t
                t_u = None
                for k in range(KC):
                    for tt in range(NTT):
                        w = []
                        if k == 0:
                            w.append(self.pfree.get(("A", 2 + tt)))
                            if tt == 0 and fl == 0:
                                w.append(tu)
                        last = (k == KC - 1 and tt == NTT - 1)
                        t = P.op("pe", lambda e, k=k, tt=tt, fl=fl, wut=wut: e.matmul(
                            self.bankA(2 + tt), wut[:, k, fl * 128:(fl + 1) * 128], xn[:, k, tt * TT:(tt + 1) * TT],
                            start=(k == 0), stop=(k == KC - 1)), waits=w, cnt=self.c_pe if last else None)
                        if last:
                            t_u = t
                for tt in range(NTT):
                    sgt = self.sg[sgi % 4]
                    t_s = P.op("act", lambda e, sgt=sgt, tt=tt: e.activation(out=sgt[:], in_=self.bankA(tt), func=AF.Silu),
                               waits=[t_g, sg_free[sgi % 4]], cnt=self.c_act)
                    self.pfree[("A", tt)] = t_s
                    ab = self.actb[g % 4]
                    t_m = P.op("dve", lambda e, sgt=sgt, tt=tt, fl=fl, ab=ab: e.tensor_tensor(
                        out=ab[:, fl, tt * TT:(tt + 1) * TT], in0=self.bankA(2 + tt), in1=sgt[:], op=ALU.mult),
                        waits=[t_u, t_s, act_free[g % 4]], cnt=self.c_dve)
                    self.pfree[("A", 2 + tt)] = t_m
                    sg_free[sgi % 4] = t_m
                    sgi += 1
                    act_ready[g].append(t_m)
                if fl == 1:
                    self.wg.free[sg_] = t_g
                    self.wu.free[su_] = t_u
            if g + 2 < NG:
                issue(g + 2)
            if g >= 2 and g % 2 == 0:
                down([g - 2, g - 1])
        if NG % 2 == 0:
            down([NG - 2, NG - 1])
        else:
            down([NG - 1])
        self.act_free = act_free
        self.sg_free = sg_free
        self.xn_free = [self.wg.free[loads[NG - 1][0]], self.wu.free[loads[NG - 1][2]]]
        self.h_tok = [dn_last[(dc, tt)] for dc in range(KC) for tt in range(NTT)]
        self.h_tok_k = {dc: [dn_last[(dc, tt)] for tt in range(NTT)] for dc in range(KC)}
        self.attn_done = []
        self.qT_free = []

    def proj_fm(self, W, evac, extra_first=(), after_prefetch=None):
        c, P = self.c, self.P
        KC, TT, NTT = c.KC, c.TT, c.NTT
        ntile = W.shape[1] // 256
        loads = {}

        self.rings_to_proj()
        PD = 4

        def issue(t):
            loads[t] = self.load_w(self.wp, self.wtile(W, t), list(extra_first) if t < PD else [])

        for t_ in range(min(PD, ntile)):
            issue(t_)
        if after_prefetch is not None:
            after_prefetch()
        t_last = None
        for t in range(ntile):
            s, tl = loads[t]
            wt = self.wp.slots[s]
            for fl in range(2):
                oc = 2 * t + fl
                st = oc % 2
                t_g = None
                for k in range(KC):
                    for tt in range(NTT):
                        bi = st * 2 + tt
                        w = []
                        if k == 0:
                            w.append(self.pfree.get(("A", bi)))
                            if tt == 0 and fl == 0:
                                w.append(tl)
                        if t == 0 and fl == 0 and tt == 0:
                            w.append(self.xn_ready_k[k] if self.xn_ready_k is not None else (self.xn_ready if k == 0 else None))
                        last = (k == KC - 1 and tt == NTT - 1)
                        tk = P.op("pe", lambda e, k=k, tt=tt, fl=fl, bi=bi, wt=wt: e.matmul(
                            self.bankA(bi), wt[:, k, fl * 128:(fl + 1) * 128], self.xn[:, k, tt * TT:(tt + 1) * TT],
                            start=(k == 0), stop=(k == KC - 1)), waits=w, cnt=self.c_pe if last else None)
                        if last:
                            t_g = tk
                for tt in range(NTT):
                    bi = st * 2 + tt
                    self.pfree[("A", bi)] = evac(oc, tt, self.bankA(bi), t_g)
                t_last = t_g
            self.wp.free[s] = t_last
            if t + PD < ntile:
                issue(t + PD)
        self.xn_free = [t_last]
        return t_last

    def proj_dram(self, W, dst, toks, after_prefetch=None):
        c, P = self.c, self.P
        TT, NTT = c.TT, c.NTT
        kst_free = getattr(self, "kst_free", [None, None])
        state = {}

        def evac(oc, tt, bank, tok):
            sl = oc % 2
            d = self.kst[sl][:, tt * TT:(tt + 1) * TT]
            w = [tok, kst_free[sl]]
            if (oc + tt) % 2 == 0:
                t = P.op("act", lambda e: e.activation(out=d, in_=bank, func=AF.Copy), waits=w, cnt=self.c_act)
            else:
                t = P.op("dve", lambda e: e.tensor_copy(out=d, in_=bank), waits=w, cnt=self.c_dve)
            state.setdefault(oc, []).append(t)
            if tt == NTT - 1:
                ts = P.op("sp", lambda e: e.dma_start(out=dst[oc * 128:(oc + 1) * 128, :], in_=self.kst[sl][:]),
                          waits=state[oc], cnt=self.c_kst[sl])
                kst_free[sl] = ts
                toks.append(ts)
            return t

        self.proj_fm(W, evac, after_prefetch=after_prefetch)
        self.kst_free = kst_free

    def proj_q(self, W, after_prefetch=None):
        self.q_st_tok = []
        self.proj_dram(W, self.qloc, self.q_st_tok, after_prefetch=after_prefetch)

    def proj_k(self, W, xi):
        self.k_st_tok = []
        self.proj_dram(W, self.kloc[xi], self.k_st_tok)

    def proj_v(self, W, xi, after_prefetch=None):
        c, P = self.c, self.P
        KC, NBL = c.KC, c.NBL
        ntile = W.shape[1] // 256
        loads = {}
        vst_free = getattr(self, "vst_free", [None, None])
        self.v_st_tok = []

        self.rings_to_proj()
        PD = 4

        def issue(t):
            loads[t] = self.load_w(self.wp, self.wtile(W, t))

        for t_ in range(min(PD, ntile)):
            issue(t_)
        if after_prefetch is not None:
            after_prefetch()
        t_g = None
        vv = self.vloc[xi].rearrange("(tb p) d -> p tb d", p=128)
        for t in range(ntile):
            s, tl = loads[t]
            wt = self.wp.slots[s]
            sl = t % 2
            evs = []
            for tb in range(NBL):
                bi = tb % 4
                for k in range(KC):
                    w = []
                    if k == 0:
                        w.append(self.pfree.get(("B", bi)))
                        if tb == 0:
                            w.append(tl)
                    if t == 0 and tb == 0:
                        w.append(self.xn_ready_k[k] if self.xn_ready_k is not None else (self.xn_ready if k == 0 else None))
                    last = k == KC - 1
                    tk = P.op("pe", lambda e, k=k, tb=tb, bi=bi, wt=wt: e.matmul(
                        self.bankB(bi, 256), self.xn[:, k, tb * 128:(tb + 1) * 128], wt[:, k, :],
                        start=(k == 0), stop=(k == KC - 1)), waits=w, cnt=self.c_pe if last else None)
                    if last:
                        t_g = tk
                dst = self.vst[sl][:, tb, :]
                w = [t_g, vst_free[sl]]
                if tb % 2 == 0:
                    te = P.op("act", lambda e, dst=dst, bi=bi: e.activation(out=dst, in_=self.bankB(bi, 256), func=AF.Copy),
                              waits=w, cnt=self.c_act)
                else:
                    te = P.op("dve", lambda e, dst=dst, bi=bi: e.tensor_copy(out=dst, in_=self.bankB(bi, 256)),
                              waits=w, cnt=self.c_dve)
                self.pfree[("B", bi)] = te
                evs.append(te)
            ts = P.op("sp", lambda e, sl=sl, t=t: e.dma_start(out=vv[:, :, t * 256:(t + 1) * 256], in_=self.vst[sl][:]),
                      waits=evs, cnt=self.c_vst[sl])
            vst_free[sl] = ts
            self.v_st_tok.append(ts)
            self.wp.free[s] = t_g
            if t + PD < ntile:
                issue(t + PD)
        self.vst_free = vst_free
        self.xn_free = [t_g]

    def exchange(self, xi, kind):
        c, P = self.c, self.P
        if kind == "k":
            w0, loc, pieces, rows = self.k_st_tok, self.kloc[xi], self.kall[xi], self.rows_k
        else:
            w0, loc, pieces, rows = self.v_st_tok, self.vloc[xi], self.vall[xi], self.rows_v
        for p, dst in enumerate(pieces):
            src = loc[p * rows:(p + 1) * rows, :]
            t = P.op("pool", lambda e, src=src, dst=dst: e.collective_compute(
                "AllGather", ALU.bypass, replica_groups=PAIRS, ins=[src.opt()], outs=[dst.opt()]),
                waits=w0 if p == 0 else [], cnt=self.c_cc)
            self.x_tok.append(t)

    def load_kq(self, xi, h, slot, extra):
        c, P = self.c, self.P
        rk = self.rows_k
        pk = (h * 128) // rk
        r0 = h * 128 - pk * rk
        src = self.kall[xi][pk].rearrange("(r x) (m q i) -> x r m q i", r=2, q=2, i=128)
        dst = self.kh[slot][:].rearrange("p (m g i) -> p m g i", g=4, i=128)
        w = self.x_tok + list(extra)
        tk = []
        for r in range(2):
            for par in range(2):
                g = r if par == 0 else 3 - r
                tk.append(P.op("sp", lambda e, r=r, par=par, g=g: e.dma_start(
                    out=dst[:, :, g, :], in_=src[r0:r0 + 128, r, :, par, :]),
                    waits=w + [self.kh_free[slot]], cnt=self.c_kh[slot]))
        tq = [P.op("sp", lambda e: e.dma_start(out=self.qh[slot][:], in_=self.qloc[h * 128:(h + 1) * 128, :]),
                   waits=self.q_st_tok + list(extra) + [self.qh_free[slot]], cnt=self.c_qh[slot])]
        return tk, tq

    def load_v(self, xi, h, slot, extra):
        c, P = self.c, self.P
        rv = self.rows_v
        w = self.x_tok + list(extra)
        tv = []
        bpp = rv // 128
        vhv = self.vh[slot][:].rearrange("p (r b) d -> p r b d", r=2)
        for pv, piece in enumerate(self.vall[xi]):
            src = piece.rearrange("(r b p) d -> p r b d", r=2, p=128)
            for r in range(2):
                tv.append(P.op("sp", lambda e, src=src, r=r, pv=pv: e.dma_start(
                    out=vhv[:, r, pv * bpp:(pv + 1) * bpp, :], in_=src[:, r, :, h * 128:(h + 1) * 128]),
                    waits=w + [self.vh_free[slot]], cnt=self.c_vh[slot]))
        return tv

    def attn_sb(self, l):
        c, P = self.c, self.P
        H, NBL, SEQ = c.H, c.NBL, c.SEQ
        scale = 128.0 ** -0.5
        psZ = self.psA
        psT = self.psB[:, 0:1024].bitcast(BF16)
        self.kh_free = [None, None]
        self.vh_free = [None, None]
        self.qh_free = [None, None]
        phase_w = list(self.xn_free) + list(self.k_st_tok) + list(self.v_st_tok) + list(self.q_st_tok)
        t0 = [P.op("dve", lambda e, i=i: e.memset(self.SP[i][:, 0:1], 0.0), waits=phase_w, cnt=self.c_dve) for i in range(2)]
        lb_order = []
        for q_ in range(NBL // 2):
            lb_order += [q_, NBL - 1 - q_]
        if NBL % 2:
            lb_order.append(NBL // 2)
        iters = [(h, lb) for h in range(H) for lb in lb_order]
        first_lb, last_lb = lb_order[0], lb_order[-1]
        lbv = lb_order[min(5, NBL - 1)]
        N = len(iters)
        kq, vv = {}, {}
        kq[0] = self.load_kq(l, 0, 0, phase_w)
        vv[0] = self.load_v(l, 0, 0, phase_w)
        e_free = [None, None, None]
        sp_free = [t0[0], t0[1]]
        c_free = [None, None]
        a_free = [None, None, None]
        at_free = [None, None]
        T = {}
        o_tok = []
        t_z_last = None
        for s in range(N + 4):
            i2 = s - 2
            if 0 <= i2 < N:
                d = T[i2]
                kl, ni = d["kl"], d["ni"]
                Eb, Cb, ab = self.E[i2 % 3], self.C[i2 % 2], self.A[i2 % 3]
                t_g = P.op("act", lambda e, kl=kl, ni=ni, Cb=Cb: e.activation(out=Cb[:, 0:kl], in_=Cb[:, 0:kl], func=AF.Exp,
                                                                               bias=self.nT[:, ni:ni + 1]), waits=[d["t_nt"]], cnt=self.c_act)
                kp = 0
                d["kp"] = kp
                t_ap = None
                d["t_g"] = t_g
                d["t_ap"] = t_ap
            if s < N:
                h, lb = iters[s]
                if lb == first_lb and h + 1 < H:
                    kq[h + 1] = self.load_kq(l, h + 1, (h + 1) % 2, phase_w)
                tk, tq = kq[h]
                kh, qh = self.kh[h % 2], self.qh[h % 2]
                kl = (2 * lb + 2) * 128
                b = s % 2
                t_z = None
                segs = []
                c0 = 0
                while c0 < kl - 256:
                    n = min(512, kl - 256 - c0)
                    segs.append((c0, n, True, True, None))
                    c0 += n
                segs.append((kl - 256, 256, True, False, None))
                segs.append((kl - 256, 256, False, True, lb % 2))
                for si, (c0, n, st_, sp_, mk) in enumerate(segs):
                    w = (tk + tq + [self.pfree.get("Z")]) if si == 0 else []
                    last = si == len(segs) - 1
                    if mk is None:
                        fn = lambda e, c0=c0, n=n, qh=qh, kh=kh, lb=lb, st_=st_, sp_=sp_: e.matmul(
                            psZ[:, c0:c0 + n], qh[:, lb * 128:(lb + 1) * 128], kh[:, c0:c0 + n], start=st_, stop=sp_)
                    else:
                        fn = lambda e, c0=c0, n=n, mk=mk: e.matmul(
                            psZ[:, c0:c0 + n], self.ident_b[:], self.mb_b[:, mk, :], start=False, stop=True)
                    t = P.op("pe", fn, waits=w, cnt=self.c_pe if last else None)
                    if last:
                        t_z = t
                if lb == last_lb:
                    self.kh_free[h % 2] = t_z
                    self.qh_free[h % 2] = t_z
                t_z_last = t_z
                Eb, SPb = self.E[s % 3], self.SP[b]
                t_e = P.op("act", lambda e, kl=kl, Eb=Eb: e.activation(out=Eb[:, 0:kl], in_=psZ[:, 0:kl], func=AF.Exp, scale=scale),
                           waits=[t_z, e_free[s % 3]], cnt=self.c_act)
                self.pfree["Z"] = t_e
                t_l = P.op("act", lambda e, kl=kl, Eb=Eb, SPb=SPb: e.activation(out=SPb[:, 1:kl + 1], in_=Eb[:, 0:kl], func=AF.Ln, bias=1.0),
                           waits=[sp_free[b]], cnt=self.c_act)
                T[s] = dict(h=h, lb=lb, kl=kl, t_l=t_l)
            j4 = s - 4
            if 0 <= j4 < N:
                d = T[j4]
                h, lb, kl = d["h"], d["lb"], d["kl"]
                nkb = 2 * lb + 2
                b = j4 % 2
                atb = self.AT[b]
                vh = self.vh[h % 2]
                tv = vv[h]
                t_cp = d["t_cp"]
                ob = 2 + b
                t_av = None
                for gb in range(nkb):
                    pos = own(gb) * NBL + gb // 2
                    last = gb == nkb - 1
                    w = [t_cp, self.pfree.get(("B", ob))] + tv if gb == 0 else []
                    t = P.op("pe", lambda e, gb=gb, pos=pos, vh=vh, atb=atb, ob=ob, nkb=nkb: e.matmul(
                        self.bankB(ob, 128), vh[:, pos, :], atb[:, gb * 128:(gb + 1) * 128], start=(gb == 0), stop=(gb == nkb - 1)),
                        waits=w, cnt=self.c_pe if last else None)
                    if last:
                        t_av = t
                at_free[b] = t_av
                if lb == last_lb:
                    self.vh_free[h % 2] = t_av
                t_o = P.op("act", lambda e, ob=ob, h=h, lb=lb: e.activation(out=self.xn[:, h, lb * 128:(lb + 1) * 128],
                                                                            in_=self.bankB(ob, 128), func=AF.Copy),
                           waits=[t_av], cnt=self.c_act)
                self.pfree[("B", ob)] = t_o
                o_tok.append(t_o)
            j = s - 3
            if 0 <= j < N:
                d = T[j]
                h, lb, kl = d["h"], d["lb"], d["kl"]
                nkb = 2 * lb + 2
                b = j % 2
                ab, atb = self.A[j % 3], self.AT[b]
                vh = self.vh[h % 2]
                tv = vv[h]
                t_t = None
                for kb in range(nkb):
                    last = kb == nkb - 1
                    t = P.op("pe", lambda e, kb=kb, ab=ab: e.transpose(psT[:, kb * 128:(kb + 1) * 128], ab[:, kb * 128:(kb + 1) * 128],
                                                                       self.ident_b[:]),
                             waits=[d["t_ma"], self.pfree.get("T")] if kb == 0 else [], cnt=self.c_pe if last else None)
                    if last:
                        t_t = t
                a_free[j % 3] = t_t
                d["t_t"] = t_t
            i = s - 1
            if 0 <= i < N:
                d = T[i]
                h, lb, kl = d["h"], d["lb"], d["kl"]
                b = i % 2
                Eb, SPb, Cb, ab = self.E[i % 3], self.SP[b], self.C[b], self.A[i % 3]
                t_sc = P.op("dve", lambda e, kl=kl, SPb=SPb, Cb=Cb: e.tensor_tensor_scan(
                    out=Cb[:, 0:kl], data0=self.one1[:, 0:1].to_broadcast([128, kl]), data1=SPb[:, 0:kl], initial=0.0,
                    op0=ALU.mult, op1=ALU.add), waits=[d["t_l"], c_free[b]], cnt=self.c_dve)
                sp_free[b] = t_sc
                ni = i % 4
                t_nt = P.op("dve", lambda e, kl=kl, ni=ni, Cb=Cb: e.tensor_scalar(out=self.nT[:, ni:ni + 1], in0=Cb[:, kl - 1:kl], scalar1=-1.0,
                                                                                  scalar2=None, op0=ALU.mult), waits=[t_sc], cnt=self.c_dve)
                d["t_nt"] = t_nt
                d["ni"] = ni
            if 0 <= i2 < N:
                d = T[i2]
                kl, kp = d["kl"], d["kp"]
                Eb, Cb, ab = self.E[i2 % 3], self.C[i2 % 2], self.A[i2 % 3]
                t_ad = P.op("dve", lambda e, kl=kl, kp=kp, Eb=Eb, Cb=Cb, ab=ab: e.tensor_tensor(out=ab[:, kp:kl], in0=Eb[:, kp:kl], in1=Cb[:, kp:kl], op=ALU.mult),
                            waits=[d["t_g"], a_free[i2 % 3]], cnt=self.c_dve)
                t_a = [t_ad]
                e_free[i2 % 3] = t_a
                c_free[i2 % 2] = t_a
                d["t_ma"] = t_a
            if 0 <= j < N:
                d = T[j]
                h, lb, kl = d["h"], d["lb"], d["kl"]
                nkb = 2 * lb + 2
                b = j % 2
                ab, atb = self.A[j % 3], self.AT[b]
                vh = self.vh[h % 2]
                tv = vv[h]
                t_t = d["t_t"]
                if j % 3 == 2:
                    t_cp = P.op("dve", lambda e, kl=kl, atb=atb: e.tensor_copy(out=atb[:, 0:kl], in_=psT[:, 0:kl]),
                                waits=[t_t, at_free[b]], cnt=self.c_dve)
                else:
                    t_cp = P.op("act", lambda e, kl=kl, atb=atb: e.activation(out=atb[:, 0:kl], in_=psT[:, 0:kl], func=AF.Copy),
                                waits=[t_t, at_free[b]], cnt=self.c_act)
                self.pfree["T"] = t_cp
                d["t_cp"] = t_cp
            if s < N:
                h_, lb_ = iters[s]
                if lb_ == lbv and h_ + 1 < H:
                    vv[h_ + 1] = self.load_v(l, h_ + 1, (h_ + 1) % 2, phase_w)
        self.attn_done = [o_tok[-1], self.pfree["T"], e_free[0], e_free[1], e_free[2], T[N - 1]["t_ma"], T[N - 2]["t_ma"]]
        self.qT_free = []
        self.xn_ready = [o_tok[-1], o_tok[-2]]
        self.xn_ready_k = None

    def attn_band(self, lbi):
        c, P = self.c, self.P
        H, NBL = c.H, c.NBL
        xi = c.NA
        scale = 128.0 ** -0.5
        if lbi == 0:
            self.kh_free = [None, None]
            self.vh_free = [None, None]
        self.qh_free = [None, None]
        phase_w = list(self.xn_free) + list(self.q_st_tok)
        if lbi == 0:
            phase_w += list(self.k_st_tok) + list(self.v_st_tok)
        bt_free = [None, None]
        s_free = [None, None]
        p_free = [None, None]
        rd_free = [None, None]

        def load(h, slot):
            tk, tq = self.load_kq(xi, h, slot, phase_w)
            tb = P.op("sp", lambda e: e.dma_start(out=self.bt[slot][:], in_=self.bt_d[lbi, h].rearrange("p (a f) -> p a f", a=2)),
                      waits=[bt_free[slot]] + phase_w, cnt=self.c_bt[slot])
            return tk + tq, tb

        iters = [(h, lb) for h in range(H) for lb in range(NBL)]
        N = len(iters)
        heads = {0: load(0, 0)}
        vv = {0: self.load_v(xi, 0, 0, phase_w)}
        T = {}
        o_tok = []
        lbv = min(2, NBL - 1)
        for s in range(N + 1):
            if s < N:
                h, lb = iters[s]
                if lb == 0 and h + 1 < H:
                    heads[h + 1] = load(h + 1, (h + 1) % 2)
                if lb == lbv and h + 1 < H:
                    vv[h + 1] = self.load_v(xi, h + 1, (h + 1) % 2, phase_w)
                tk, tb = heads[h]
                kh, bt = self.kh[h % 2], self.bt[h % 2]
                sl = s % 2
                j0 = max(0, 4 - 2 * lb)
                qs = self.qh[h % 2][:, lb * 128:(lb + 1) * 128]
                psS = self.psA[:, sl * 1024:sl * 1024 + 768]
                t_s = None
                for j in range(j0, 6):
                    kb = 2 * lb - 4 + j
                    last = j == 5
                    w = tk + [self.pfree.get(("S", sl))] if j == j0 else []
                    t = P.op("pe", lambda e, j=j, kb=kb, qs=qs, kh=kh, psS=psS: e.matmul(
                        psS[:, j * 128:(j + 1) * 128], kh[:, kb * 128:(kb + 1) * 128], qs, start=True, stop=True),
                        waits=w, cnt=self.c_pe if last else None)
                    if last:
                        t_s = t
                if lb == NBL - 1:
                    self.kh_free[h % 2] = t_s
                    self.qh_free[h % 2] = t_s
                Sb, Pb = self.S[sl], self.Pb[sl]
                t_b = P.op("dve", lambda e, j0=j0, psS=psS, Sb=Sb, bt=bt, lb=lb: e.scalar_tensor_tensor(
                    out=Sb[:, j0 * 128:768], in0=psS[:, j0 * 128:768], scalar=scale, in1=bt[:, lb % 2, j0 * 128:768],
                    op0=ALU.mult, op1=ALU.add), waits=[t_s, tb, s_free[sl]], cnt=self.c_dve)
                self.pfree[("S", sl)] = t_b
                if lb == NBL - 1:
                    bt_free[h % 2] = t_b
                t_e = P.op("act", lambda e, j0=j0, Sb=Sb, Pb=Pb: e.activation(out=Pb[:, j0 * 128:768], in_=Sb[:, j0 * 128:768], func=AF.Exp),
                           waits=[t_b, p_free[sl]], cnt=self.c_act)
                s_free[sl] = t_e
                T[s] = dict(h=h, lb=lb, j0=j0, t_e=t_e)
            i = s - 1
            if 0 <= i < N:
                d = T[i]
                h, lb, j0, t_e = d["h"], d["lb"], d["j0"], d["t_e"]
                sl = i % 2
                Pb = self.Pb[sl]
                vh = self.vh[h % 2]
                tv = vv[h]
                t_d = None
                for j in range(j0, 6):
                    last = j == 5
                    t = P.op("pe", lambda e, j=j, Pb=Pb, sl=sl, j0=j0: e.matmul(
                        self.bankB(sl, 128), self.ones_b[:], Pb[:, j * 128:(j + 1) * 128], start=(j == j0), stop=(j == 5)),
                        waits=[t_e, self.pfree.get(("B", sl))] if j == j0 else [], cnt=self.c_pe if last else None)
                    if last:
                        t_d = t
                t_o = None
                for j in range(j0, 6):
                    kb = 2 * lb - 4 + j
                    pos = own(kb) * NBL + kb // 2
                    last = j == 5
                    t = P.op("pe", lambda e, j=j, pos=pos, Pb=Pb, sl=sl, vh=vh, j0=j0: e.matmul(
                        self.bankB(2 + sl, 128), vh[:, pos, :], Pb[:, j * 128:(j + 1) * 128], start=(j == j0), stop=(j == 5)),
                        waits=tv + [self.pfree.get(("B", 2 + sl))] if j == j0 else [], cnt=self.c_pe if last else None)
                    if last:
                        t_o = t
                p_free[sl] = t_o
                if lb == NBL - 1:
                    self.vh_free[h % 2] = t_o
                rd = self.rden[sl]
                t_ln = P.op("act", lambda e, rd=rd, sl=sl: e.activation(out=rd[:], in_=self.bankB(sl, 128), func=AF.Ln),
                            waits=[t_d, rd_free[sl]], cnt=self.c_act)
                self.pfree[("B", sl)] = t_ln
                t_r = P.op("act", lambda e, rd=rd: e.activation(out=rd[:], in_=rd[:], func=AF.Exp, scale=-1.0), cnt=self.c_act)
                t_st = P.op("dve", lambda e, rd=rd, sl=sl, h=h, lb=lb: e.tensor_tensor(
                    out=self.xn[:, h, lb * 128:(lb + 1) * 128], in0=self.bankB(2 + sl, 128), in1=rd[:], op=ALU.mult),
                    waits=[t_o, t_r], cnt=self.c_dve)
                self.pfree[("B", 2 + sl)] = t_st
                rd_free[sl] = t_st
                o_tok.append(t_st)
        self.attn_done = [o_tok[-1], s_free[0], s_free[1]]
        self.qT_free = []
        self.xn_ready = [o_tok[-1]]
        self.xn_ready_k = None

    def oproj(self, W):
        c, P = self.c, self.P
        TT = c.TT
        h_new = {}

        def evac(oc, tt, bank, tok):
            t = P.op("dve", lambda e: e.tensor_tensor(out=self.hT[:, oc, tt * TT:(tt + 1) * TT], in0=bank,
                                                      in1=self.hT[:, oc, tt * TT:(tt + 1) * TT], op=ALU.add),
                     waits=[tok], cnt=self.c_dve)
            h_new[(oc, tt)] = t
            return t

        self.proj_fm(W, evac, extra_first=self.attn_done)
        self.h_tok = list(h_new.values())
        self.h_tok_k = {oc: [h_new[(oc, tt)] for tt in range(self.c.NTT)] for oc in range(self.c.KC)}


def host_prep(cfg, inputs):
    c = cfg
    f32 = np.float32
    x = np.asarray(inputs["x"], f32)
    gvecs = []
    g_ffn = np.asarray(inputs["g_ffn"], f32)
    g_mix = np.asarray(inputs["g_mix"], f32)
    for l in range(c.DEPTH):
        for j in range(2):
            gvecs.append(g_ffn[l, j])
    for l in range(c.DEPTH):
        gvecs.append(g_mix[l])
    gvecs.append(np.asarray(inputs["g_kv"], f32))
    gvecs.append(np.asarray(inputs["g_final"], f32))
    gv = np.stack([g.reshape(c.KC, 128).T for g in gvecs], axis=1).reshape(128, c.NV * c.KC)
    gv = np.ascontiguousarray(gv, f32)
    cst = np.zeros((128, 512), f32)
    cst[:, 0:128] = 1.0
    cst[:, 256:384] = np.eye(128, dtype=f32)
    tq = np.arange(128)[:, None]
    ts = np.arange(128)[None, :]
    tri = (ts < tq).astype(f32)
    ones = np.ones((128, 128), f32)
    zeros = np.zeros((128, 128), f32)
    mask_even = np.concatenate([tri, zeros], axis=1)
    mask_odd = np.concatenate([ones, tri], axis=1)
    rel_bias = np.asarray(inputs["rel_bias_b"], f32)
    kk = np.arange(128)[:, None]
    qq = np.arange(128)[None, :]
    bt_par = []
    for par in range(2):
        tiles_idx = []
        tiles_valid = []
        for j in range(6):
            off = par + 4 - j
            rel = off * 128 + qq - kk
            idx = np.clip(rel, -REL_CLIP, REL_CLIP) + REL_CLIP
            dchunk = -2 * off + (kk >= 64).astype(np.int64) - (qq >= 64).astype(np.int64)
            valid = (dchunk >= -LEFT_CHUNKS) & (dchunk <= 0) & (off >= 0) & (off <= 4)
            tiles_idx.append(idx)
            tiles_valid.append(valid)
        bt_par.append((np.stack(tiles_idx), np.stack(tiles_valid)))
    in_maps = []
    shared = {k: np.ascontiguousarray(np.asarray(inputs[k], f32)) for k in
              ("w_ffn_gate", "w_ffn_up", "w_ffn_down", "w_qkv_a", "w_o_a", "w_kv_shared", "w_q_b", "w_o_b")}
    bt_rank = []
    for r in range(2):
        per_l = []
        for lbi in range(c.NBD):
            rb = rel_bias[lbi]
            arr = np.empty((c.H, 128, 2, 6, 128), f32)
            for pi in range(2):
                idx, valid = bt_par[(pi + r) % 2]
                g = rb[:, idx]
                g = np.where(valid[None], g, f32(NEG))
                arr[:, :, pi, :, :] = np.transpose(g, (0, 2, 1, 3))
            per_l.append(arr.reshape(c.H, 128, 2 * 6 * 128))
        bt_rank.append(np.ascontiguousarray(np.stack(per_l)))
    m2_rank = []
    mb_rank = []
    for r in range(2):
        mm = [mask_even, mask_odd] if r == 0 else [mask_odd, mask_even]
        m2_rank.append(np.ascontiguousarray(np.concatenate(mm, axis=1)))
        mb_rank.append(np.ascontiguousarray(np.where(m2_rank[-1] > 0.5, f32(0.0), f32(-30000.0)).astype(f32)))
    toks = []
    for r in range(2):
        idx = np.concatenate([np.arange(gblock(r, lb) * 128, gblock(r, lb) * 128 + 128) for lb in range(c.NBL)])
        toks.append(idx)
    for core in range(2 * c.BATCH):
        b, r = core // 2, core % 2
        m = dict(shared)
        m["xT"] = np.ascontiguousarray(x[b][toks[r], :].T)
        m["gv"] = gv
        m["cst"] = cst
        m["m2"] = m2_rank[r]
        m["mb"] = mb_rank[r]
        m["bt"] = bt_rank[r]
        in_maps.append(m)
    return in_maps, toks


def host_post(cfg, results, toks):
    c = cfg
    out = np.empty((c.BATCH, c.SEQ, c.D), np.float32)
    for core in range(2 * c.BATCH):
        b, r = core // 2, core % 2
        out[b, toks[r], :] = np.asarray(results[core]["outT"]).T
    return out


_CACHE = {}


def kernel(**inputs):
    cfg = Cfg()
    if "nc" not in _CACHE:
        _CACHE["nc"] = Builder(cfg).build()
    nc = _CACHE["nc"]
    in_maps, toks = host_prep(cfg, inputs)
    res = run_bass_kernel_spmd(nc, in_maps, core_ids=list(range(8)))
    return host_post(cfg, res.results, toks)
```

```python
import numpy as np
from contextlib import ExitStack
import concourse.bass as bass
import concourse.mybir as mybir
from concourse.bass_utils import run_bass_kernel_spmd

F32 = mybir.dt.float32
BF16 = mybir.dt.bfloat16
AF = mybir.ActivationFunctionType
ALU = mybir.AluOpType

EPS = 1e-6
CHUNK = 64
LEFT_CHUNKS = 8
REL_CLIP = 256
NEG = -1.0e30
PAIRS = [[0, 1], [2, 3], [4, 5], [6, 7]]


class Cfg:
    def __init__(self, H=16, DFF=5632, SEQ=2048, DEPTH=4, BATCH=4):
        self.H = H
        self.D = 128 * H
        self.KC = H
        self.DFF = DFF
        self.FC = DFF // 128
        self.NG = self.FC // 2
        self.SEQ = SEQ
        self.OWN = SEQ // 2
        self.NBL = self.OWN // 128
        self.NB = SEQ // 128
        self.TT = min(512, self.OWN)
        self.NTT = self.OWN // self.TT
        self.DEPTH = DEPTH
        self.NA = DEPTH // 2
        self.NBD = DEPTH - self.NA
        self.BATCH = BATCH
        self.NV = 3 * DEPTH + 2


def own(gb):
    return ((gb + 1) // 2) % 2


def gblock(r, lb):
    return 2 * lb + ((lb + r) % 2)


class Counter:
    LIMIT = 16000

    def __init__(self, prog, step):
        self.prog = prog
        self.step = step
        self.h = prog.new_sem()
        self.v = 0

    def bump(self):
        if self.v + self.step > self.LIMIT:
            self.h = self.prog.new_sem()
            self.v = 0
        self.v += self.step
        return (self.h, self.v, self.step)


class Prog:
    ENG = ("pe", "act", "dve", "pool", "sp")

    def __init__(self, nc, es):
        self.nc = nc
        self.es = es
        self.streams = {e: [] for e in self.ENG}
        self.nsem = 0

    def new_sem(self):
        self.nsem += 1
        return self.es.enter_context(self.nc.semaphore(f"sm{self.nsem}"))

    def op(self, eng, fn, waits=(), cnt=None):
        tok = cnt.bump() if cnt is not None else None
        ws = []

        def flat(w):
            if w is None:
                return
            if isinstance(w, list):
                for x in w:
                    flat(x)
            else:
                ws.append(w)

        flat(list(waits))
        self.streams[eng].append((ws, fn, tok))
        return tok

    def emit(self, name, eng):
        seen = {}
        for ws, fn, tok in self.streams[name]:
            for (h, v, _s) in ws:
                k = id(h)
                if seen.get(k, 0) < v:
                    eng.wait_ge(h, v)
                    seen[k] = v
            ins = fn(eng)
            if tok is not None:
                ins.then_inc(tok[0], tok[2])


class Ring:
    def __init__(self, slots, prog=None):
        self.slots = slots
        self.free = [None] * len(slots)
        self.i = 0
        self.cnt = [Counter(prog, 16) for _ in slots] if prog is not None else None

    def take(self):
        s = self.i % len(self.slots)
        self.i += 1
        return s


class Builder:
    def __init__(self, cfg):
        self.c = cfg

    def build(self):
        c = self.c
        nc = bass.Bass("TRN2", target_bir_lowering=False)
        self.nc = nc
        D, DFF, OWN, KC, H = c.D, c.DFF, c.OWN, c.KC, c.H

        def din(name, shape, dt=F32):
            return nc.dram_tensor(name, list(shape), dt, kind="ExternalInput").ap()

        self.xT = din("xT", [D, OWN])
        self.gv_d = din("gv", [128, c.NV * KC])
        self.cst_d = din("cst", [128, 512])
        self.m2_d = din("m2", [128, 2 * 256])
        self.mb_d = din("mb", [128, 2 * 256])
        self.bt_d = din("bt", [c.NBD, H, 128, 2 * 6 * 128])
        self.w_gate = din("w_ffn_gate", [c.DEPTH, 2, D, DFF])
        self.w_up = din("w_ffn_up", [c.DEPTH, 2, D, DFF])
        self.w_down = din("w_ffn_down", [c.DEPTH, 2, DFF, D])
        self.w_qkv = din("w_qkv_a", [c.NA, D, 3 * D])
        self.w_oa = din("w_o_a", [c.NA, D, D])
        self.w_kv = din("w_kv_shared", [D, 2 * D])
        self.w_qb = din("w_q_b", [c.NBD, D, D])
        self.w_ob = din("w_o_b", [c.NBD, D, D])
        self.outT = nc.dram_tensor("outT", [D, OWN], F32, kind="ExternalOutput").ap()
        self.dbg_on = getattr(self, "dbg_on", False)
        if self.dbg_on:
            self.dbg_d = nc.dram_tensor("dbg", [128, 8192], F32, kind="ExternalOutput").ap()
            self.dbg_col = 0
            self.dbg_map = {}
            self.dbg_toks = []

        self.NX = c.NA + 1
        self.kloc = [nc.dram_tensor(f"kloc{i}", [D, OWN], BF16, kind="Internal").ap() for i in range(self.NX)]
        self.vloc = [nc.dram_tensor(f"vloc{i}", [OWN, D], BF16, kind="Internal").ap() for i in range(self.NX)]
        self.qloc = nc.dram_tensor("qloc", [D, OWN], BF16, kind="Internal").ap()
        self.rows_k = max(128, min(D, (2 << 20) // (OWN * 2)))
        self.rows_v = max(128, min(OWN, (2 << 20) // (D * 2)))
        self.kall = [[nc.dram_tensor(f"kall{i}_{p}", [2 * self.rows_k, OWN], BF16, kind="Internal").ap()
                      for p in range(D // self.rows_k)] for i in range(self.NX)]
        self.vall = [[nc.dram_tensor(f"vall{i}_{p}", [2 * self.rows_v, D], BF16, kind="Internal").ap()
                      for p in range(OWN // self.rows_v)] for i in range(self.NX)]

        with ExitStack() as es:
            self.es = es
            self.P = Prog(nc, es)
            self.alloc()
            self.program()
            block = es.enter_context(nc.Block())
            P = self.P

            @block.tensor
            def _(e):
                P.emit("pe", e)

            @block.scalar
            def _(e):
                P.emit("act", e)

            @block.vector
            def _(e):
                P.emit("dve", e)

            @block.gpsimd
            def _(e):
                P.emit("pool", e)

            @block.sync
            def _(e):
                P.emit("sp", e)
        return nc

    def alloc(self):
        c, nc, P = self.c, self.nc, self.P
        KC, OWN, TT, H, SEQ = c.KC, c.OWN, c.TT, c.H, c.SEQ
        self.off = 16640

        def sb(name, shape, dt, at=None):
            size = int(np.prod(shape[1:])) * (4 if dt == F32 else 2)
            if at is None:
                at = self.off
                self.off = (at + size + 63) // 64 * 64
            t = nc.alloc_sbuf_tensor_at(name, list(shape), dt, offset=at)
            return t, at + size

        self.ones_f, _ = sb("ones_f", [128, 128], F32)
        self.ident_f, _ = sb("ident_f", [128, 128], F32)
        self.ones_b, _ = sb("ones_b", [128, 128], BF16)
        self.ident_b, _ = sb("ident_b", [128, 128], BF16)
        self.m2_f, _ = sb("m2_f", [128, 2, 256], F32)
        self.m2_b, _ = sb("m2_b", [128, 2, 256], BF16)
        self.mb_b, _ = sb("mb_b", [128, 2, 256], BF16)
        self.gv, _ = sb("gv", [128, c.NV * KC], F32)
        self.nT, _ = sb("nT", [128, 4], F32)
        self.one1, _ = sb("one1", [128, 4], F32)
        self.hT, _ = sb("hT", [128, KC, OWN], F32)
        self.xn, _ = sb("xn", [128, KC, OWN], BF16)
        R0 = self.off
        self.R0 = R0
        o = R0
        self.rstd, o = sb("rstd", [128, OWN], F32, at=o)
        self.sq = []
        self.sqb = []
        for i in range(2):
            tb_, _ = sb(f"sqb{i}", [128, OWN], BF16, at=o)
            self.sqb.append(tb_)
            t, o = sb(f"sq{i}", [128, OWN], F32, at=o)
            self.sq.append(t)
        self.actb = []
        act_at = o
        for i in range(4):
            t, o = sb(f"act{i}", [128, 2, OWN], BF16, at=o)
            self.actb.append(t)
        self.sg = []
        sg_at = o
        for i in range(4):
            t, o = sb(f"sg{i}", [128, TT], F32, at=o)
            self.sg.append(t)
        wg, wu, wd = [], [], []
        for i in range(2):
            t, o = sb(f"wg{i}", [128, KC, 256], BF16, at=o)
            wg.append(t)
        for i in range(2):
            t, o = sb(f"wu{i}", [128, KC, 256], BF16, at=o)
            wu.append(t)
        for i in range(4):
            t, o = sb(f"wd{i}", [128, 2, c.D], BF16, at=o)
            wd.append(t)
        self.wg, self.wu, self.wd = Ring(wg, P), Ring(wu, P), Ring(wd, P)
        self.wp = Ring(wg + wu, P)
        ffn_end = o
        self.kst = [nc.alloc_sbuf_tensor_at(f"kst{i}", [128, OWN], BF16, offset=sg_at + i * OWN * 2)
                    for i in range(2)]
        assert 2 * OWN * 2 <= 4 * TT * 4
        self.vst = [nc.alloc_sbuf_tensor_at(f"vst{i}", [128, c.NBL, 256], BF16, offset=act_at + i * 2 * OWN * 2)
                    for i in range(2)]
        assert c.NBL * 256 * 2 <= 2 * OWN * 2
        o = R0
        self.kh, self.vh, self.qh = [], [], []
        for i in range(2):
            t, o = sb(f"kh{i}", [128, SEQ], BF16, at=o)
            self.kh.append(t)
        for i in range(2):
            t, o = sb(f"vh{i}", [128, c.NB, 128], BF16, at=o)
            self.vh.append(t)
        for i in range(2):
            t, o = sb(f"qh{i}", [128, OWN], BF16, at=o)
            self.qh.append(t)
        o_common = o
        self.E, self.SP, self.C, self.A, self.AT = [], [], [], [], []
        for i in range(3):
            t, o = sb(f"E{i}", [128, SEQ], F32, at=o)
            self.E.append(t)
        for i in range(2):
            t, o = sb(f"SP{i}", [128, SEQ + 16], F32, at=o)
            self.SP.append(t)
        for i in range(2):
            t, o = sb(f"C{i}", [128, SEQ], F32, at=o)
            self.C.append(t)
        for i in range(3):
            t, o = sb(f"A{i}", [128, SEQ], BF16, at=o)
            self.A.append(t)
        for i in range(2):
            t, o = sb(f"AT{i}", [128, SEQ], BF16, at=o)
            self.AT.append(t)
        sb_end = o
        o = o_common
        self.bt = []
        for i in range(2):
            t, o = sb(f"bt{i}", [128, 2, 768], F32, at=o)
            self.bt.append(t)
        self.S = []
        for i in range(2):
            t, o = sb(f"S{i}", [128, 768], F32, at=o)
            self.S.append(t)
        self.Pb = []
        for i in range(2):
            t, o = sb(f"Pb{i}", [128, 768], BF16, at=o)
            self.Pb.append(t)
        self.rden = []
        for i in range(2):
            t, o = sb(f"rden{i}", [128, 128], F32, at=o)
            self.rden.append(t)
        band_end = o
        self.sb_total = max(sb_end, band_end, ffn_end)
        assert self.sb_total <= 229344, self.sb_total
        self.psA = self.es.enter_context(nc.psum_tensor("psA", [128, 2048], F32))
        self.psB = self.es.enter_context(nc.psum_tensor("psB", [128, 2048], F32))
        self.pfree = {}
        self.c_pe = Counter(P, 1)
        self.c_act = Counter(P, 1)
        self.c_dve = Counter(P, 1)
        self.c_w = Counter(P, 16)
        self.c_ld = Counter(P, 16)
        self.c_st = Counter(P, 16)
        self.c_dbg = Counter(P, 16)
        self.c_cc = Counter(P, 1)
        self.c_pool = Counter(P, 1)
        self.c_qh = [Counter(P, 16) for _ in range(2)]
        self.c_kh = [Counter(P, 16) for _ in range(2)]
        self.c_vh = [Counter(P, 16) for _ in range(2)]
        self.c_bt = [Counter(P, 16) for _ in range(2)]
        self.c_kst = [Counter(P, 16) for _ in range(2)]
        self.c_vst = [Counter(P, 16) for _ in range(2)]

    def bankA(self, i, n=None):
        n = self.c.TT if n is None else n
        return self.psA[:, i * 512:i * 512 + n]

    def bankB(self, i, n=None):
        n = self.c.TT if n is None else n
        return self.psB[:, i * 512:i * 512 + n]

    def program(self):
        c, P = self.c, self.P
        KC, OWN = c.KC, c.OWN
        toks = []
        cst = self.cst_d
        toks.append(P.op("sp", lambda e: e.dma_start(out=self.ones_f[:], in_=cst[:, 0:128]), cnt=self.c_ld))
        toks.append(P.op("sp", lambda e: e.dma_start(out=self.ident_f[:], in_=cst[:, 256:384]), cnt=self.c_ld))
        toks.append(P.op("sp", lambda e: e.dma_start(out=self.gv[:], in_=self.gv_d), cnt=self.c_ld))
        toks.append(P.op("sp", lambda e: e.dma_start(out=self.m2_f[:], in_=self.m2_d.rearrange("p (a b) -> p a b", a=2)), cnt=self.c_ld))
        toks.append(P.op("pool", lambda e: e.dma_start(out=self.ones_b[:], in_=cst[:, 0:128]), cnt=self.c_w))
        toks.append(P.op("pool", lambda e: e.dma_start(out=self.ident_b[:], in_=cst[:, 256:384]), cnt=self.c_w))
        toks.append(P.op("pool", lambda e: e.dma_start(out=self.m2_b[:], in_=self.m2_d.rearrange("p (a b) -> p a b", a=2)), cnt=self.c_w))
        toks.append(P.op("pool", lambda e: e.dma_start(out=self.mb_b[:], in_=self.mb_d.rearrange("p (a b) -> p a b", a=2)), cnt=self.c_w))
        t1 = P.op("dve", lambda e: e.memset(self.one1[:], 1.0), cnt=self.c_dve)
        self.const_tok = toks + [t1]
        xv = self.xT.rearrange("(c p) t -> p c t", p=128)
        self.h_tok = []
        self.c_x = [Counter(P, 16) for _ in range(KC)]
        hk = {}
        for g in range(KC):
            q = "sp" if g % 2 == 0 else "act"
            t = P.op(q, lambda e, g=g: e.dma_start(out=self.hT[:, g, :], in_=xv[:, g, :]), cnt=self.c_x[g])
            self.h_tok.append(t)
            hk[g] = [t] + (self.const_tok if g == 0 else [])
        self.h_tok_k = hk
        self.h_tok += self.const_tok
        self.xn_free = []
        self.rstd_free = None
        self.sq_free = [None, None]
        self.attn_done = []
        self.qT_free = []

        nst = getattr(self, "nstages", 99)
        si = 0
        for l in range(c.DEPTH):
            if si >= nst:
                break
            self.ffn(l, 0)
            si += 1
            if si >= nst:
                break
            if l < c.NA:
                self.rmsnorm(c.DEPTH * 2 + l)
                self.x_tok = []
                self.proj_k(self.w_qkv[l][:, c.D:2 * c.D], l)
                self.proj_v(self.w_qkv[l][:, 2 * c.D:3 * c.D], l, after_prefetch=lambda l=l: self.exchange(l, "k"))
                self.proj_q(self.w_qkv[l][:, 0:c.D], after_prefetch=lambda l=l: self.exchange(l, "v"))
                self.attn_sb(l)
                self.oproj(self.w_oa[l])
            else:
                lb = l - c.NA
                if lb == 0:
                    self.rmsnorm(3 * c.DEPTH)
                    self.x_tok = []
                    self.proj_k(self.w_kv[:, 0:c.D], c.NA)
                    self.proj_v(self.w_kv[:, c.D:2 * c.D], c.NA, after_prefetch=lambda: self.exchange(c.NA, "k"))
                    self.pending_xv = True
                self.rmsnorm(c.DEPTH * 2 + l)
                if getattr(self, "pending_xv", False):
                    self.pending_xv = False
                    self.proj_q(self.w_qb[lb], after_prefetch=lambda: self.exchange(c.NA, "v"))
                else:
                    self.proj_q(self.w_qb[lb])
                self.attn_band(lb)
                self.oproj(self.w_ob[lb])
            si += 1
            if si >= nst:
                break
            self.ffn(l, 1)
            si += 1
        if si >= nst and nst < 3 * c.DEPTH + 1:
            self.dump_h()
        else:
            self.final_norm(3 * c.DEPTH + 1)

    def dbg(self, name, ap, n, waits):
        if not self.dbg_on:
            return
        c0 = self.dbg_col
        self.dbg_col += n
        self.dbg_map[name] = (c0, n)
        t = self.P.op("pool", lambda e: e.dma_start(out=self.dbg_d[:, c0:c0 + n], in_=ap), waits=waits, cnt=self.c_dbg)
        self.dbg_toks.append(t)
        return t

    def dump_h(self):
        P = self.P
        ov = self.outT.rearrange("(c p) t -> p c t", p=128)
        t_o = P.op("sp", lambda e: e.dma_start(out=ov, in_=self.hT[:]), waits=[self.h_tok], cnt=self.c_st)
        for name in Prog.ENG:
            P.op(name, lambda e: e.wait_ge(t_o[0], t_o[1]))
        if self.dbg_on:
            for t in self.dbg_toks:
                P.op("sp", lambda e, t=t: e.wait_ge(t[0], t[1]))

    def rings_to_proj(self):
        if getattr(self, "ring_mode", "ffn") == "proj":
            return
        self.ring_mode = "proj"
        self.wp.free = [self.wg.free[0], self.wg.free[1], self.wu.free[0], self.wu.free[1]]

    def rings_to_ffn(self):
        if getattr(self, "ring_mode", "ffn") == "ffn":
            return
        self.ring_mode = "ffn"
        allp = [t for t in self.wp.free]
        self.wg.free = [[self.wg.free[0], allp[0]], [self.wg.free[1], allp[1]]]
        self.wu.free = [[self.wu.free[0], allp[2]], [self.wu.free[1], allp[3]]]

    def load_w(self, ring, src, extra=()):
        P = self.P
        s = ring.take()
        dst = ring.slots[s]
        tok = P.op("pool", lambda e: e.dma_start(out=dst[:], in_=src), waits=[ring.free[s]] + list(extra), cnt=ring.cnt[s])
        return s, tok

    def wtile(self, W, t):
        return W.rearrange("(c p) f -> p c f", p=128)[:, :, t * 256:(t + 1) * 256]

    def rmsnorm(self, gi, to_out=False):
        c, P = self.c, self.P
        KC, TT, NTT, D = c.KC, c.TT, c.NTT, c.D
        hT, xn, rstd = self.hT, self.xn, self.rstd
        hk = getattr(self, "h_tok_k", None)
        t_mm = None
        for k in range(KC):
            sqb = self.sqb[k % 2]
            wk = hk[k] if hk is not None else (self.h_tok if k == 0 else None)
            t_sq = P.op("act", lambda e, sqb=sqb, k=k: e.activation(out=sqb[:], in_=hT[:, k, :], func=AF.Square),
                        waits=[wk, self.sq_free[k % 2]], cnt=self.c_act)
            for tt in range(NTT):
                last = tt == NTT - 1
                w = [t_sq]
                if k == 0:
                    w.append(self.pfree.get(("B", tt)))
                t = P.op("pe", lambda e, sqb=sqb, tt=tt, k=k: e.matmul(
                    self.bankB(tt), self.ones_b[:], sqb[:, tt * TT:(tt + 1) * TT], start=(k == 0), stop=(k == KC - 1)),
                    waits=w, cnt=self.c_pe if last else None)
                if last:
                    t_mm = t
            self.sq_free[k % 2] = t_mm
        t_r = None
        for tt in range(NTT):
            t_r = P.op("act", lambda e, tt=tt: e.activation(out=rstd[:, tt * TT:(tt + 1) * TT], in_=self.bankB(tt),
                                                            func=AF.Ln, scale=1.0 / D, bias=EPS),
                       waits=[t_mm, self.rstd_free], cnt=self.c_act)
            self.pfree[("B", tt)] = t_r
        t_rs = P.op("act", lambda e: e.activation(out=rstd[:], in_=rstd[:], func=AF.Exp, scale=-0.5), cnt=self.c_act)
        KD = KC
        dst = hT if to_out else xn
        toks = {}
        first = {"dve": True, "pool": True}
        for k in range(KC):
            eng = "dve" if k < KD else "pool"
            fn = lambda e, k=k: e.scalar_tensor_tensor(out=dst[:, k, :], in0=hT[:, k, :],
                                                       scalar=self.gv[:, gi * KC + k:gi * KC + k + 1],
                                                       in1=rstd[:], op0=ALU.mult, op1=ALU.mult)
            w = [t_rs, self.h_tok, self.xn_free] if first[eng] else []
            first[eng] = False
            toks[k] = P.op(eng, fn, waits=w, cnt=self.c_dve if eng == "dve" else self.c_pool)
        last_d = toks[KD - 1]
        last_p = toks[KC - 1] if KD < KC else None
        self.rstd_free = [last_d, last_p]
        self.xn_ready = [last_d, last_p]
        self.xn_ready_k = toks
        self.xn_free = []
        self.h_tok_k = None
        return [last_d, last_p]

    def final_norm(self, gi):
        P = self.P
        self.rmsnorm(gi, to_out=True)
        ov = self.outT.rearrange("(c p) t -> p c t", p=128)
        toks = self.xn_ready_k
        outs = []
        for k in range(self.c.KC):
            outs.append(P.op("sp", lambda e, k=k: e.dma_start(out=ov[:, k, :], in_=self.hT[:, k, :]), waits=[toks[k]], cnt=self.c_st))
        t_o = outs[-1]
        for name in Prog.ENG:
            P.op(name, lambda e: e.wait_ge(t_o[0], t_o[1]))
        if self.dbg_on:
            for t in self.dbg_toks:
                P.op("sp", lambda e, t=t: e.wait_ge(t[0], t[1]))

    def ffn(self, l, j):
        c, P = self.c, self.P
        KC, TT, NTT, NG, OWN = c.KC, c.TT, c.NTT, c.NG, c.OWN
        hT, xn = self.hT, self.xn
        self.rmsnorm(l * 2 + j)
        self.rings_to_ffn()
        Wg, Wu, Wd = self.w_gate[l, j], self.w_up[l, j], self.w_down[l, j]
        wdv = Wd.rearrange("(c p) d -> p c d", p=128)
        extra_first = list(self.attn_done)
        loads = {}

        loads_d = {}

        def issue(g):
            ex = extra_first if g < 2 else []
            sg_, tg = self.load_w(self.wg, self.wtile(Wg, g), ex)
            su_, tu = self.load_w(self.wu, self.wtile(Wu, g), ex)
            loads[g] = (sg_, tg, su_, tu)

        def issue_d(g):
            ex = extra_first if g < 2 else []
            loads_d[g] = self.load_w(self.wd, wdv[:, 2 * g:2 * g + 2, :], ex + list(self.qT_free))

        issue(0)
        if NG > 1:
            issue(1)
        for g_ in range(min(4, NG)):
            issue_d(g_)
        act_ready = {}
        act_free = getattr(self, "act_free", [None] * 4)
        sg_free = getattr(self, "sg_free", [None] * 4)
        sgi = 0
        h_new = []
        dn_last = {}

        def down(gs):
            t_pe = None
            nmm = 2 * len(gs)
            for dc in range(KC):
                st = dc % 2
                mi = 0
                for g in gs:
                    sd_, td = loads_d[g]
                    wdt = self.wd.slots[sd_]
                    ab = self.actb[g % 4]
                    for fl in range(2):
                        for tt in range(NTT):
                            bi = st * 2 + tt if NTT == 2 else st
                            w = []
                            if mi == 0:
                                w.append(self.pfree.get(("B", bi)))
                            if dc == 0 and fl == 0 and tt == 0:
                                w += [td] + act_ready[g]
                            last = (mi == nmm - 1 and tt == NTT - 1)
                            t = P.op("pe", lambda e, bi=bi, fl=fl, dc=dc, tt=tt, wdt=wdt, ab=ab, mi=mi: e.matmul(
                                self.bankB(bi), wdt[:, fl, dc * 128:(dc + 1) * 128], ab[:, fl, tt * TT:(tt + 1) * TT],
                                start=(mi == 0), stop=(mi == nmm - 1)), waits=w, cnt=self.c_pe if last else None)
                            if last:
                                t_pe = t
                        mi += 1
                for tt in range(NTT):
                    bi = st * 2 + tt if NTT == 2 else st
                    t_acc = P.op("dve", lambda e, bi=bi, dc=dc, tt=tt: e.scalar_tensor_tensor(
                        out=hT[:, dc, tt * TT:(tt + 1) * TT], in0=self.bankB(bi), scalar=0.5,
                        in1=hT[:, dc, tt * TT:(tt + 1) * TT], op0=ALU.mult, op1=ALU.add),
                        waits=[t_pe], cnt=self.c_dve)
                    self.pfree[("B", bi)] = t_acc
                    dn_last[(dc, tt)] = t_acc
            for g in gs:
                self.wd.free[loads_d[g][0]] = t_pe
                act_free[g % 4] = t_pe
            for g in gs:
                if g + 4 < NG:
                    issue_d(g + 4)

        for g in range(NG):
            sg_, tg, su_, tu = loads[g]
            wgt, wut = self.wg.slots[sg_], self.wu.slots[su_]
            act_ready[g] = []
            for fl in range(2):
                t_g = None
                for k in range(KC):
                    for tt in range(NTT):
                        w = []
                        if k == 0:
                            w.append(self.pfree.get(("A", tt)))
                            if tt == 0 and fl == 0:
                                w.append(tg)
                        if g == 0 and fl == 0 and tt == 0:
                            w.append(self.xn_ready_k[k] if self.xn_ready_k is not None else (self.xn_ready if k == 0 else None))
                        last = (k == KC - 1 and tt == NTT - 1)
                        t = P.op("pe", lambda e, k=k, tt=tt, fl=fl, wgt=wgt: e.matmul(
                            self.bankA(tt), wgt[:, k, fl * 128:(fl + 1) * 128], xn[:, k, tt * TT:(tt + 1) * TT],
                            start=(k == 0), stop=(k == KC - 1)), waits=w, cnt=self.c_pe if last else None)
                        if last:
                            t_g = t
                t_u = None
                for k in range(KC):
                    for tt in range(NTT):
                        w = []
                        if k == 0:
                            w.append(self.pfree.get(("A", 2 + tt)))
                            if tt == 0 and fl == 0:
                                w.append(tu)
                        last = (k == KC - 1 and tt == NTT - 1)
                        t = P.op("pe", lambda e, k=k, tt=tt, fl=fl, wut=wut: e.matmul(
                            self.bankA(2 + tt), wut[:, k, fl * 128:(fl + 1) * 128], xn[:, k, tt * TT:(tt + 1) * TT],
                            start=(k == 0), stop=(k == KC - 1)), waits=w, cnt=self.c_pe if last else None)
                        if last:
                            t_u = t
                for tt in range(NTT):
                    sgt = self.sg[sgi % 4]
                    t_s = P.op("act", lambda e, sgt=sgt, tt=tt: e.activation(out=sgt[:], in_=self.bankA(tt), func=AF.Silu),
                               waits=[t_g, sg_free[sgi % 4]], cnt=self.c_act)
                    self.pfree[("A", tt)] = t_s
                    ab = self.actb[g % 4]
                    t_m = P.op("dve", lambda e, sgt=sgt, tt=tt, fl=fl, ab=ab: e.tensor_tensor(
                        out=ab[:, fl, tt * TT:(tt + 1) * TT], in0=self.bankA(2 + tt), in1=sgt[:], op=ALU.mult),
                        waits=[t_u, t_s, act_free[g % 4]], cnt=self.c_dve)
                    self.pfree[("A", 2 + tt)] = t_m
                    sg_free[sgi % 4] = t_m
                    sgi += 1
                    act_ready[g].append(t_m)
                if fl == 1:
                    self.wg.free[sg_] = t_g
                    self.wu.free[su_] = t_u
            if g + 2 < NG:
                issue(g + 2)
            if g >= 2 and g % 2 == 0:
                down([g - 2, g - 1])
        if NG % 2 == 0:
            down([NG - 2, NG - 1])
        else:
            down([NG - 1])
        self.act_free = act_free
        self.sg_free = sg_free
        self.xn_free = [self.wg.free[loads[NG - 1][0]], self.wu.free[loads[NG - 1][2]]]
        self.h_tok = [dn_last[(dc, tt)] for dc in range(KC) for tt in range(NTT)]
        self.h_tok_k = {dc: [dn_last[(dc, tt)] for tt in range(NTT)] for dc in range(KC)}
        self.attn_done = []
        self.qT_free = []

    def proj_fm(self, W, evac, extra_first=(), after_prefetch=None):
        c, P = self.c, self.P
        KC, TT, NTT = c.KC, c.TT, c.NTT
        ntile = W.shape[1] // 256
        loads = {}

        self.rings_to_proj()
        PD = 4

        def issue(t):
            loads[t] = self.load_w(self.wp, self.wtile(W, t), list(extra_first) if t < PD else [])

        for t_ in range(min(PD, ntile)):
            issue(t_)
        if after_prefetch is not None:
            after_prefetch()
        t_last = None
        for t in range(ntile):
            s, tl = loads[t]
            wt = self.wp.slots[s]
            for fl in range(2):
                oc = 2 * t + fl
                st = oc % 2
                t_g = None
                for k in range(KC):
                    for tt in range(NTT):
                        bi = st * 2 + tt
                        w = []
                        if k == 0:
                            w.append(self.pfree.get(("A", bi)))
                            if tt == 0 and fl == 0:
                                w.append(tl)
                        if t == 0 and fl == 0 and tt == 0:
                            w.append(self.xn_ready_k[k] if self.xn_ready_k is not None else (self.xn_ready if k == 0 else None))
                        last = (k == KC - 1 and tt == NTT - 1)
                        tk = P.op("pe", lambda e, k=k, tt=tt, fl=fl, bi=bi, wt=wt: e.matmul(
                            self.bankA(bi), wt[:, k, fl * 128:(fl + 1) * 128], self.xn[:, k, tt * TT:(tt + 1) * TT],
                            start=(k == 0), stop=(k == KC - 1)), waits=w, cnt=self.c_pe if last else None)
                        if last:
                            t_g = tk
                for tt in range(NTT):
                    bi = st * 2 + tt
                    self.pfree[("A", bi)] = evac(oc, tt, self.bankA(bi), t_g)
                t_last = t_g
            self.wp.free[s] = t_last
            if t + PD < ntile:
                issue(t + PD)
        self.xn_free = [t_last]
        return t_last

    def proj_dram(self, W, dst, toks, after_prefetch=None):
        c, P = self.c, self.P
        TT, NTT = c.TT, c.NTT
        kst_free = getattr(self, "kst_free", [None, None])
        state = {}

        def evac(oc, tt, bank, tok):
            sl = oc % 2
            d = self.kst[sl][:, tt * TT:(tt + 1) * TT]
            w = [tok, kst_free[sl]]
            if (oc + tt) % 2 == 0:
                t = P.op("act", lambda e: e.activation(out=d, in_=bank, func=AF.Copy), waits=w, cnt=self.c_act)
            else:
                t = P.op("dve", lambda e: e.tensor_copy(out=d, in_=bank), waits=w, cnt=self.c_dve)
            state.setdefault(oc, []).append(t)
            if tt == NTT - 1:
                ts = P.op("sp", lambda e: e.dma_start(out=dst[oc * 128:(oc + 1) * 128, :], in_=self.kst[sl][:]),
                          waits=state[oc], cnt=self.c_kst[sl])
                kst_free[sl] = ts
                toks.append(ts)
            return t

        self.proj_fm(W, evac, after_prefetch=after_prefetch)
        self.kst_free = kst_free

    def proj_q(self, W, after_prefetch=None):
        self.q_st_tok = []
        self.proj_dram(W, self.qloc, self.q_st_tok, after_prefetch=after_prefetch)

    def proj_k(self, W, xi):
        self.k_st_tok = []
        self.proj_dram(W, self.kloc[xi], self.k_st_tok)

    def proj_v(self, W, xi, after_prefetch=None):
        c, P = self.c, self.P
        KC, NBL = c.KC, c.NBL
        ntile = W.shape[1] // 256
        loads = {}
        vst_free = getattr(self, "vst_free", [None, None])
        self.v_st_tok = []

        self.rings_to_proj()
        PD = 4

        def issue(t):
            loads[t] = self.load_w(self.wp, self.wtile(W, t))

        for t_ in range(min(PD, ntile)):
            issue(t_)
        if after_prefetch is not None:
            after_prefetch()
        t_g = None
        vv = self.vloc[xi].rearrange("(tb p) d -> p tb d", p=128)
        for t in range(ntile):
            s, tl = loads[t]
            wt = self.wp.slots[s]
            sl = t % 2
            evs = []
            for tb in range(NBL):
                bi = tb % 4
                for k in range(KC):
                    w = []
                    if k == 0:
                        w.append(self.pfree.get(("B", bi)))
                        if tb == 0:
                            w.append(tl)
                    if t == 0 and tb == 0:
                        w.append(self.xn_ready_k[k] if self.xn_ready_k is not None else (self.xn_ready if k == 0 else None))
                    last = k == KC - 1
                    tk = P.op("pe", lambda e, k=k, tb=tb, bi=bi, wt=wt: e.matmul(
                        self.bankB(bi, 256), self.xn[:, k, tb * 128:(tb + 1) * 128], wt[:, k, :],
                        start=(k == 0), stop=(k == KC - 1)), waits=w, cnt=self.c_pe if last else None)
                    if last:
                        t_g = tk
                dst = self.vst[sl][:, tb, :]
                w = [t_g, vst_free[sl]]
                if tb % 2 == 0:
                    te = P.op("act", lambda e, dst=dst, bi=bi: e.activation(out=dst, in_=self.bankB(bi, 256), func=AF.Copy),
                              waits=w, cnt=self.c_act)
                else:
                    te = P.op("dve", lambda e, dst=dst, bi=bi: e.tensor_copy(out=dst, in_=self.bankB(bi, 256)),
                              waits=w, cnt=self.c_dve)
                self.pfree[("B", bi)] = te
                evs.append(te)
            ts = P.op("sp", lambda e, sl=sl, t=t: e.dma_start(out=vv[:, :, t * 256:(t + 1) * 256], in_=self.vst[sl][:]),
                      waits=evs, cnt=self.c_vst[sl])
            vst_free[sl] = ts
            self.v_st_tok.append(ts)
            self.wp.free[s] = t_g
            if t + PD < ntile:
                issue(t + PD)
        self.vst_free = vst_free
        self.xn_free = [t_g]

    def exchange(self, xi, kind):
        c, P = self.c, self.P
        if kind == "k":
            w0, loc, pieces, rows = self.k_st_tok, self.kloc[xi], self.kall[xi], self.rows_k
        else:
            w0, loc, pieces, rows = self.v_st_tok, self.vloc[xi], self.vall[xi], self.rows_v
        for p, dst in enumerate(pieces):
            src = loc[p * rows:(p + 1) * rows, :]
            t = P.op("pool", lambda e, src=src, dst=dst: e.collective_compute(
                "AllGather", ALU.bypass, replica_groups=PAIRS, ins=[src.opt()], outs=[dst.opt()]),
                waits=w0 if p == 0 else [], cnt=self.c_cc)
            self.x_tok.append(t)

    def load_kq(self, xi, h, slot, extra):
        c, P = self.c, self.P
        rk = self.rows_k
        pk = (h * 128) // rk
        r0 = h * 128 - pk * rk
        src = self.kall[xi][pk].rearrange("(r x) (m q i) -> x r m q i", r=2, q=2, i=128)
        dst = self.kh[slot][:].rearrange("p (m g i) -> p m g i", g=4, i=128)
        w = self.x_tok + list(extra)
        tk = []
        for r in range(2):
            for par in range(2):
                g = r if par == 0 else 3 - r
                tk.append(P.op("sp", lambda e, r=r, par=par, g=g: e.dma_start(
                    out=dst[:, :, g, :], in_=src[r0:r0 + 128, r, :, par, :]),
                    waits=w + [self.kh_free[slot]], cnt=self.c_kh[slot]))
        tq = [P.op("sp", lambda e: e.dma_start(out=self.qh[slot][:], in_=self.qloc[h * 128:(h + 1) * 128, :]),
                   waits=self.q_st_tok + list(extra) + [self.qh_free[slot]], cnt=self.c_qh[slot])]
        return tk, tq

    def load_v(self, xi, h, slot, extra):
        c, P = self.c, self.P
        rv = self.rows_v
        w = self.x_tok + list(extra)
        tv = []
        bpp = rv // 128
        vhv = self.vh[slot][:].rearrange("p (r b) d -> p r b d", r=2)
        for pv, piece in enumerate(self.vall[xi]):
            src = piece.rearrange("(r b p) d -> p r b d", r=2, p=128)
            for r in range(2):
                tv.append(P.op("sp", lambda e, src=src, r=r, pv=pv: e.dma_start(
                    out=vhv[:, r, pv * bpp:(pv + 1) * bpp, :], in_=src[:, r, :, h * 128:(h + 1) * 128]),
                    waits=w + [self.vh_free[slot]], cnt=self.c_vh[slot]))
        return tv

    def attn_sb(self, l):
        c, P = self.c, self.P
        H, NBL, SEQ = c.H, c.NBL, c.SEQ
        scale = 128.0 ** -0.5
        psZ = self.psA
        psT = self.psB[:, 0:1024].bitcast(BF16)
        self.kh_free = [None, None]
        self.vh_free = [None, None]
        self.qh_free = [None, None]
        phase_w = list(self.xn_free) + list(self.k_st_tok) + list(self.v_st_tok) + list(self.q_st_tok)
        t0 = [P.op("dve", lambda e, i=i: e.memset(self.SP[i][:, 0:1], 0.0), waits=phase_w, cnt=self.c_dve) for i in range(2)]
        lb_order = []
        for q_ in range(NBL // 2):
            lb_order += [q_, NBL - 1 - q_]
        if NBL % 2:
            lb_order.append(NBL // 2)
        iters = [(h, lb) for h in range(H) for lb in lb_order]
        first_lb, last_lb = lb_order[0], lb_order[-1]
        lbv = lb_order[min(5, NBL - 1)]
        N = len(iters)
        kq, vv = {}, {}
        kq[0] = self.load_kq(l, 0, 0, phase_w)
        vv[0] = self.load_v(l, 0, 0, phase_w)
        e_free = [None, None, None]
        sp_free = [t0[0], t0[1]]
        c_free = [None, None]
        a_free = [None, None, None]
        at_free = [None, None]
        T = {}
        o_tok = []
        t_z_last = None
        for s in range(N + 4):
            i2 = s - 2
            if 0 <= i2 < N:
                d = T[i2]
                kl, ni = d["kl"], d["ni"]
                Eb, Cb, ab = self.E[i2 % 3], self.C[i2 % 2], self.A[i2 % 3]
                t_g = P.op("act", lambda e, kl=kl, ni=ni, Cb=Cb: e.activation(out=Cb[:, 0:kl], in_=Cb[:, 0:kl], func=AF.Exp,
                                                                               bias=self.nT[:, ni:ni + 1]), waits=[d["t_nt"]], cnt=self.c_act)
                kp = 0
                d["kp"] = kp
                t_ap = None
                d["t_g"] = t_g
                d["t_ap"] = t_ap
            if s < N:
                h, lb = iters[s]
                if lb == first_lb and h + 1 < H:
                    kq[h + 1] = self.load_kq(l, h + 1, (h + 1) % 2, phase_w)
                tk, tq = kq[h]
                kh, qh = self.kh[h % 2], self.qh[h % 2]
                kl = (2 * lb + 2) * 128
                b = s % 2
                t_z = None
                segs = []
                c0 = 0
                while c0 < kl - 256:
                    n = min(512, kl - 256 - c0)
                    segs.append((c0, n, True, True, None))
                    c0 += n
                segs.append((kl - 256, 256, True, False, None))
                segs.append((kl - 256, 256, False, True, lb % 2))
                for si, (c0, n, st_, sp_, mk) in enumerate(segs):
                    w = (tk + tq + [self.pfree.get("Z")]) if si == 0 else []
                    last = si == len(segs) - 1
                    if mk is None:
                        fn = lambda e, c0=c0, n=n, qh=qh, kh=kh, lb=lb, st_=st_, sp_=sp_: e.matmul(
                            psZ[:, c0:c0 + n], qh[:, lb * 128:(lb + 1) * 128], kh[:, c0:c0 + n], start=st_, stop=sp_)
                    else:
                        fn = lambda e, c0=c0, n=n, mk=mk: e.matmul(
                            psZ[:, c0:c0 + n], self.ident_b[:], self.mb_b[:, mk, :], start=False, stop=True)
                    t = P.op("pe", fn, waits=w, cnt=self.c_pe if last else None)
                    if last:
                        t_z = t
                if lb == last_lb:
                    self.kh_free[h % 2] = t_z
                    self.qh_free[h % 2] = t_z
                t_z_last = t_z
                Eb, SPb = self.E[s % 3], self.SP[b]
                t_e = P.op("act", lambda e, kl=kl, Eb=Eb: e.activation(out=Eb[:, 0:kl], in_=psZ[:, 0:kl], func=AF.Exp, scale=scale),
                           waits=[t_z, e_free[s % 3]], cnt=self.c_act)
                self.pfree["Z"] = t_e
                t_l = P.op("act", lambda e, kl=kl, Eb=Eb, SPb=SPb: e.activation(out=SPb[:, 1:kl + 1], in_=Eb[:, 0:kl], func=AF.Ln, bias=1.0),
                           waits=[sp_free[b]], cnt=self.c_act)
                T[s] = dict(h=h, lb=lb, kl=kl, t_l=t_l)
            j4 = s - 4
            if 0 <= j4 < N:
                d = T[j4]
                h, lb, kl = d["h"], d["lb"], d["kl"]
                nkb = 2 * lb + 2
                b = j4 % 2
                atb = self.AT[b]
                vh = self.vh[h % 2]
                tv = vv[h]
                t_cp = d["t_cp"]
                ob = 2 + b
                t_av = None
                for gb in range(nkb):
                    pos = own(gb) * NBL + gb // 2
                    last = gb == nkb - 1
                    w = [t_cp, self.pfree.get(("B", ob))] + tv if gb == 0 else []
                    t = P.op("pe", lambda e, gb=gb, pos=pos, vh=vh, atb=atb, ob=ob, nkb=nkb: e.matmul(
                        self.bankB(ob, 128), vh[:, pos, :], atb[:, gb * 128:(gb + 1) * 128], start=(gb == 0), stop=(gb == nkb - 1)),
                        waits=w, cnt=self.c_pe if last else None)
                    if last:
                        t_av = t
                at_free[b] = t_av
                if lb == last_lb:
                    self.vh_free[h % 2] = t_av
                t_o = P.op("act", lambda e, ob=ob, h=h, lb=lb: e.activation(out=self.xn[:, h, lb * 128:(lb + 1) * 128],
                                                                            in_=self.bankB(ob, 128), func=AF.Copy),
                           waits=[t_av], cnt=self.c_act)
                self.pfree[("B", ob)] = t_o
                o_tok.append(t_o)
            j = s - 3
            if 0 <= j < N:
                d = T[j]
                h, lb, kl = d["h"], d["lb"], d["kl"]
                nkb = 2 * lb + 2
                b = j % 2
                ab, atb = self.A[j % 3], self.AT[b]
                vh = self.vh[h % 2]
                tv = vv[h]
                t_t = None
                for kb in range(nkb):
                    last = kb == nkb - 1
                    t = P.op("pe", lambda e, kb=kb, ab=ab: e.transpose(psT[:, kb * 128:(kb + 1) * 128], ab[:, kb * 128:(kb + 1) * 128],
                                                                       self.ident_b[:]),
                             waits=[d["t_ma"], self.pfree.get("T")] if kb == 0 else [], cnt=self.c_pe if last else None)
                    if last:
                        t_t = t
                a_free[j % 3] = t_t
                d["t_t"] = t_t
            i = s - 1
            if 0 <= i < N:
                d = T[i]
                h, lb, kl = d["h"], d["lb"], d["kl"]
                b = i % 2
                Eb, SPb, Cb, ab = self.E[i % 3], self.SP[b], self.C[b], self.A[i % 3]
                t_sc = P.op("dve", lambda e, kl=kl, SPb=SPb, Cb=Cb: e.tensor_tensor_scan(
                    out=Cb[:, 0:kl], data0=self.one1[:, 0:1].to_broadcast([128, kl]), data1=SPb[:, 0:kl], initial=0.0,
                    op0=ALU.mult, op1=ALU.add), waits=[d["t_l"], c_free[b]], cnt=self.c_dve)
                sp_free[b] = t_sc
                ni = i % 4
                t_nt = P.op("dve", lambda e, kl=kl, ni=ni, Cb=Cb: e.tensor_scalar(out=self.nT[:, ni:ni + 1], in0=Cb[:, kl - 1:kl], scalar1=-1.0,
                                                                                  scalar2=None, op0=ALU.mult), waits=[t_sc], cnt=self.c_dve)
                d["t_nt"] = t_nt
                d["ni"] = ni
            if 0 <= i2 < N:
                d = T[i2]
                kl, kp = d["kl"], d["kp"]
                Eb, Cb, ab = self.E[i2 % 3], self.C[i2 % 2], self.A[i2 % 3]
                t_ad = P.op("dve", lambda e, kl=kl, kp=kp, Eb=Eb, Cb=Cb, ab=ab: e.tensor_tensor(out=ab[:, kp:kl], in0=Eb[:, kp:kl], in1=Cb[:, kp:kl], op=ALU.mult),
                            waits=[d["t_g"], a_free[i2 % 3]], cnt=self.c_dve)
                t_a = [t_ad]
                e_free[i2 % 3] = t_a
                c_free[i2 % 2] = t_a
                d["t_ma"] = t_a
            if 0 <= j < N:
                d = T[j]
                h, lb, kl = d["h"], d["lb"], d["kl"]
                nkb = 2 * lb + 2
                b = j % 2
                ab, atb = self.A[j % 3], self.AT[b]
                vh = self.vh[h % 2]
                tv = vv[h]
                t_t = d["t_t"]
                if j % 2 == 1:
                    t_cp = P.op("dve", lambda e, kl=kl, atb=atb: e.tensor_copy(out=atb[:, 0:kl], in_=psT[:, 0:kl]),
                                waits=[t_t, at_free[b]], cnt=self.c_dve)
                else:
                    t_cp = P.op("act", lambda e, kl=kl, atb=atb: e.activation(out=atb[:, 0:kl], in_=psT[:, 0:kl], func=AF.Copy),
                                waits=[t_t, at_free[b]], cnt=self.c_act)
                self.pfree["T"] = t_cp
                d["t_cp"] = t_cp
            if s < N:
                h_, lb_ = iters[s]
                if lb_ == lbv and h_ + 1 < H:
                    vv[h_ + 1] = self.load_v(l, h_ + 1, (h_ + 1) % 2, phase_w)
        self.attn_done = [o_tok[-1], self.pfree["T"], e_free[0], e_free[1], e_free[2], T[N - 1]["t_ma"], T[N - 2]["t_ma"]]
        self.qT_free = []
        self.xn_ready = [o_tok[-1], o_tok[-2]]
        self.xn_ready_k = None

    def attn_band(self, lbi):
        c, P = self.c, self.P
        H, NBL = c.H, c.NBL
        xi = c.NA
        scale = 128.0 ** -0.5
        if lbi == 0:
            self.kh_free = [None, None]
            self.vh_free = [None, None]
        self.qh_free = [None, None]
        phase_w = list(self.xn_free) + list(self.q_st_tok)
        if lbi == 0:
            phase_w += list(self.k_st_tok) + list(self.v_st_tok)
        bt_free = [None, None]
        s_free = [None, None]
        p_free = [None, None]
        rd_free = [None, None]

        def load(h, slot):
            tk, tq = self.load_kq(xi, h, slot, phase_w)
            tb = P.op("sp", lambda e: e.dma_start(out=self.bt[slot][:], in_=self.bt_d[lbi, h].rearrange("p (a f) -> p a f", a=2)),
                      waits=[bt_free[slot]] + phase_w, cnt=self.c_bt[slot])
            return tk + tq, tb

        iters = [(h, lb) for h in range(H) for lb in range(NBL)]
        N = len(iters)
        heads = {0: load(0, 0)}
        vv = {0: self.load_v(xi, 0, 0, phase_w)}
        T = {}
        o_tok = []
        lbv = min(2, NBL - 1)
        for s in range(N + 1):
            if s < N:
                h, lb = iters[s]
                if lb == 0 and h + 1 < H:
                    heads[h + 1] = load(h + 1, (h + 1) % 2)
                if lb == lbv and h + 1 < H:
                    vv[h + 1] = self.load_v(xi, h + 1, (h + 1) % 2, phase_w)
                tk, tb = heads[h]
                kh, bt = self.kh[h % 2], self.bt[h % 2]
                sl = s % 2
                j0 = max(0, 4 - 2 * lb)
                qs = self.qh[h % 2][:, lb * 128:(lb + 1) * 128]
                psS = self.psA[:, sl * 1024:sl * 1024 + 768]
                t_s = None
                for j in range(j0, 6):
                    kb = 2 * lb - 4 + j
                    last = j == 5
                    w = tk + [self.pfree.get(("S", sl))] if j == j0 else []
                    t = P.op("pe", lambda e, j=j, kb=kb, qs=qs, kh=kh, psS=psS: e.matmul(
                        psS[:, j * 128:(j + 1) * 128], kh[:, kb * 128:(kb + 1) * 128], qs, start=True, stop=True),
                        waits=w, cnt=self.c_pe if last else None)
                    if last:
                        t_s = t
                if lb == NBL - 1:
                    self.kh_free[h % 2] = t_s
                    self.qh_free[h % 2] = t_s
                Sb, Pb = self.S[sl], self.Pb[sl]
                t_b = P.op("dve", lambda e, j0=j0, psS=psS, Sb=Sb, bt=bt, lb=lb: e.scalar_tensor_tensor(
                    out=Sb[:, j0 * 128:768], in0=psS[:, j0 * 128:768], scalar=scale, in1=bt[:, lb % 2, j0 * 128:768],
                    op0=ALU.mult, op1=ALU.add), waits=[t_s, tb, s_free[sl]], cnt=self.c_dve)
                self.pfree[("S", sl)] = t_b
                if lb == NBL - 1:
                    bt_free[h % 2] = t_b
                t_e = P.op("act", lambda e, j0=j0, Sb=Sb, Pb=Pb: e.activation(out=Pb[:, j0 * 128:768], in_=Sb[:, j0 * 128:768], func=AF.Exp),
                           waits=[t_b, p_free[sl]], cnt=self.c_act)
                s_free[sl] = t_e
                T[s] = dict(h=h, lb=lb, j0=j0, t_e=t_e)
            i = s - 1
            if 0 <= i < N:
                d = T[i]
                h, lb, j0, t_e = d["h"], d["lb"], d["j0"], d["t_e"]
                sl = i % 2
                Pb = self.Pb[sl]
                vh = self.vh[h % 2]
                tv = vv[h]
                t_d = None
                for j in range(j0, 6):
                    last = j == 5
                    t = P.op("pe", lambda e, j=j, Pb=Pb, sl=sl, j0=j0: e.matmul(
                        self.bankB(sl, 128), self.ones_b[:], Pb[:, j * 128:(j + 1) * 128], start=(j == j0), stop=(j == 5)),
                        waits=[t_e, self.pfree.get(("B", sl))] if j == j0 else [], cnt=self.c_pe if last else None)
                    if last:
                        t_d = t
                t_o = None
                for j in range(j0, 6):
                    kb = 2 * lb - 4 + j
                    pos = own(kb) * NBL + kb // 2
                    last = j == 5
                    t = P.op("pe", lambda e, j=j, pos=pos, Pb=Pb, sl=sl, vh=vh, j0=j0: e.matmul(
                        self.bankB(2 + sl, 128), vh[:, pos, :], Pb[:, j * 128:(j + 1) * 128], start=(j == j0), stop=(j == 5)),
                        waits=tv + [self.pfree.get(("B", 2 + sl))] if j == j0 else [], cnt=self.c_pe if last else None)
                    if last:
                        t_o = t
                p_free[sl] = t_o
                if lb == NBL - 1:
                    self.vh_free[h % 2] = t_o
                rd = self.rden[sl]
                t_ln = P.op("act", lambda e, rd=rd, sl=sl: e.activation(out=rd[:], in_=self.bankB(sl, 128), func=AF.Ln),
                            waits=[t_d, rd_free[sl]], cnt=self.c_act)
                self.pfree[("B", sl)] = t_ln
                t_r = P.op("act", lambda e, rd=rd: e.activation(out=rd[:], in_=rd[:], func=AF.Exp, scale=-1.0), cnt=self.c_act)
                t_st = P.op("dve", lambda e, rd=rd, sl=sl, h=h, lb=lb: e.tensor_tensor(
                    out=self.xn[:, h, lb * 128:(lb + 1) * 128], in0=self.bankB(2 + sl, 128), in1=rd[:], op=ALU.mult),
                    waits=[t_o, t_r], cnt=self.c_dve)
                self.pfree[("B", 2 + sl)] = t_st
                rd_free[sl] = t_st
                o_tok.append(t_st)
        self.attn_done = [o_tok[-1], s_free[0], s_free[1]]
        self.qT_free = []
        self.xn_ready = [o_tok[-1]]
        self.xn_ready_k = None

    def oproj(self, W):
        c, P = self.c, self.P
        TT = c.TT
        h_new = {}

        def evac(oc, tt, bank, tok):
            t = P.op("dve", lambda e: e.tensor_tensor(out=self.hT[:, oc, tt * TT:(tt + 1) * TT], in0=bank,
                                                      in1=self.hT[:, oc, tt * TT:(tt + 1) * TT], op=ALU.add),
                     waits=[tok], cnt=self.c_dve)
            h_new[(oc, tt)] = t
            return t

        self.proj_fm(W, evac, extra_first=self.attn_done)
        self.h_tok = list(h_new.values())
        self.h_tok_k = {oc: [h_new[(oc, tt)] for tt in range(self.c.NTT)] for oc in range(self.c.KC)}


def host_prep(cfg, inputs):
    c = cfg
    f32 = np.float32
    x = np.asarray(inputs["x"], f32)
    gvecs = []
    g_ffn = np.asarray(inputs["g_ffn"], f32)
    g_mix = np.asarray(inputs["g_mix"], f32)
    for l in range(c.DEPTH):
        for j in range(2):
            gvecs.append(g_ffn[l, j])
    for l in range(c.DEPTH):
        gvecs.append(g_mix[l])
    gvecs.append(np.asarray(inputs["g_kv"], f32))
    gvecs.append(np.asarray(inputs["g_final"], f32))
    gv = np.stack([g.reshape(c.KC, 128).T for g in gvecs], axis=1).reshape(128, c.NV * c.KC)
    gv = np.ascontiguousarray(gv, f32)
    cst = np.zeros((128, 512), f32)
    cst[:, 0:128] = 1.0
    cst[:, 256:384] = np.eye(128, dtype=f32)
    tq = np.arange(128)[:, None]
    ts = np.arange(128)[None, :]
    tri = (ts < tq).astype(f32)
    ones = np.ones((128, 128), f32)
    zeros = np.zeros((128, 128), f32)
    mask_even = np.concatenate([tri, zeros], axis=1)
    mask_odd = np.concatenate([ones, tri], axis=1)
    rel_bias = np.asarray(inputs["rel_bias_b"], f32)
    kk = np.arange(128)[:, None]
    qq = np.arange(128)[None, :]
    bt_par = []
    for par in range(2):
        tiles_idx = []
        tiles_valid = []
        for j in range(6):
            off = par + 4 - j
            rel = off * 128 + qq - kk
            idx = np.clip(rel, -REL_CLIP, REL_CLIP) + REL_CLIP
            dchunk = -2 * off + (kk >= 64).astype(np.int64) - (qq >= 64).astype(np.int64)
            valid = (dchunk >= -LEFT_CHUNKS) & (dchunk <= 0) & (off >= 0) & (off <= 4)
            tiles_idx.append(idx)
            tiles_valid.append(valid)
        bt_par.append((np.stack(tiles_idx), np.stack(tiles_valid)))
    in_maps = []
    shared = {k: np.ascontiguousarray(np.asarray(inputs[k], f32)) for k in
              ("w_ffn_gate", "w_ffn_up", "w_ffn_down", "w_qkv_a", "w_o_a", "w_kv_shared", "w_q_b", "w_o_b")}
    bt_rank = []
    for r in range(2):
        per_l = []
        for lbi in range(c.NBD):
            rb = rel_bias[lbi]
            arr = np.empty((c.H, 128, 2, 6, 128), f32)
            for pi in range(2):
                idx, valid = bt_par[(pi + r) % 2]
                g = rb[:, idx]
                g = np.where(valid[None], g, f32(NEG))
                arr[:, :, pi, :, :] = np.transpose(g, (0, 2, 1, 3))
            per_l.append(arr.reshape(c.H, 128, 2 * 6 * 128))
        bt_rank.append(np.ascontiguousarray(np.stack(per_l)))
    m2_rank = []
    mb_rank = []
    for r in range(2):
        mm = [mask_even, mask_odd] if r == 0 else [mask_odd, mask_even]
        m2_rank.append(np.ascontiguousarray(np.concatenate(mm, axis=1)))
        mb_rank.append(np.ascontiguousarray(np.where(m2_rank[-1] > 0.5, f32(0.0), f32(-30000.0)).astype(f32)))
    toks = []
    for r in range(2):
        idx = np.concatenate([np.arange(gblock(r, lb) * 128, gblock(r, lb) * 128 + 128) for lb in range(c.NBL)])
        toks.append(idx)
    for core in range(2 * c.BATCH):
        b, r = core // 2, core % 2
        m = dict(shared)
        m["xT"] = np.ascontiguousarray(x[b][toks[r], :].T)
        m["gv"] = gv
        m["cst"] = cst
        m["m2"] = m2_rank[r]
        m["mb"] = mb_rank[r]
        m["bt"] = bt_rank[r]
        in_maps.append(m)
    return in_maps, toks


def host_post(cfg, results, toks):
    c = cfg
    out = np.empty((c.BATCH, c.SEQ, c.D), np.float32)
    for core in range(2 * c.BATCH):
        b, r = core // 2, core % 2
        out[b, toks[r], :] = np.asarray(results[core]["outT"]).T
    return out


_CACHE = {}


def kernel(**inputs):
    cfg = Cfg()
    if "nc" not in _CACHE:
        _CACHE["nc"] = Builder(cfg).build()
    nc = _CACHE["nc"]
    in_maps, toks = host_prep(cfg, inputs)
    res = run_bass_kernel_spmd(nc, in_maps, core_ids=list(range(8)))
    return host_post(cfg, res.results, toks)
```

```python
import numpy as np
from contextlib import ExitStack
import concourse.bass as bass
import concourse.mybir as mybir
from concourse.bass_utils import run_bass_kernel_spmd

F32 = mybir.dt.float32
BF16 = mybir.dt.bfloat16
AF = mybir.ActivationFunctionType
ALU = mybir.AluOpType

EPS = 1e-6
CHUNK = 64
LEFT_CHUNKS = 8
REL_CLIP = 256
NEG = -1.0e30
PAIRS = [[0, 1], [2, 3], [4, 5], [6, 7]]


class Cfg:
    def __init__(self, H=16, DFF=5632, SEQ=2048, DEPTH=4, BATCH=4):
        self.H = H
        self.D = 128 * H
        self.KC = H
        self.DFF = DFF
        self.FC = DFF // 128
        self.NG = self.FC // 2
        self.SEQ = SEQ
        self.OWN = SEQ // 2
        self.NBL = self.OWN // 128
        self.NB = SEQ // 128
        self.TT = min(512, self.OWN)
        self.NTT = self.OWN // self.TT
        self.DEPTH = DEPTH
        self.NA = DEPTH // 2
        self.NBD = DEPTH - self.NA
        self.BATCH = BATCH
        self.NV = 3 * DEPTH + 2


def own(gb):
    return ((gb + 1) // 2) % 2


def gblock(r, lb):
    return 2 * lb + ((lb + r) % 2)


class Counter:
    LIMIT = 16000

    def __init__(self, prog, step):
        self.prog = prog
        self.step = step
        self.h = prog.new_sem()
        self.v = 0

    def bump(self):
        if self.v + self.step > self.LIMIT:
            self.h = self.prog.new_sem()
            self.v = 0
        self.v += self.step
        return (self.h, self.v, self.step)


class Prog:
    ENG = ("pe", "act", "dve", "pool", "sp")

    def __init__(self, nc, es):
        self.nc = nc
        self.es = es
        self.streams = {e: [] for e in self.ENG}
        self.nsem = 0

    def new_sem(self):
        self.nsem += 1
        return self.es.enter_context(self.nc.semaphore(f"sm{self.nsem}"))

    def op(self, eng, fn, waits=(), cnt=None):
        tok = cnt.bump() if cnt is not None else None
        ws = []

        def flat(w):
            if w is None:
                return
            if isinstance(w, list):
                for x in w:
                    flat(x)
            else:
                ws.append(w)

        flat(list(waits))
        self.streams[eng].append((ws, fn, tok))
        return tok

    def emit(self, name, eng):
        seen = {}
        for ws, fn, tok in self.streams[name]:
            for (h, v, _s) in ws:
                k = id(h)
                if seen.get(k, 0) < v:
                    eng.wait_ge(h, v)
                    seen[k] = v
            ins = fn(eng)
            if tok is not None:
                ins.then_inc(tok[0], tok[2])


class Ring:
    def __init__(self, slots, prog=None):
        self.slots = slots
        self.free = [None] * len(slots)
        self.i = 0
        self.cnt = [Counter(prog, 16) for _ in slots] if prog is not None else None

    def take(self):
        s = self.i % len(self.slots)
        self.i += 1
        return s


class Builder:
    def __init__(self, cfg):
        self.c = cfg

    def build(self):
        c = self.c
        nc = bass.Bass("TRN2", target_bir_lowering=False)
        self.nc = nc
        D, DFF, OWN, KC, H = c.D, c.DFF, c.OWN, c.KC, c.H

        def din(name, shape, dt=F32):
            return nc.dram_tensor(name, list(shape), dt, kind="ExternalInput").ap()

        self.xT = din("xT", [D, OWN])
        self.gv_d = din("gv", [128, c.NV * KC])
        self.cst_d = din("cst", [128, 512])
        self.m2_d = din("m2", [128, 2 * 256])
        self.mb_d = din("mb", [128, 2 * 256])
        self.bt_d = din("bt", [c.NBD, H, 128, 2 * 6 * 128])
        self.w_gate = din("w_ffn_gate", [c.DEPTH, 2, D, DFF])
        self.w_up = din("w_ffn_up", [c.DEPTH, 2, D, DFF])
        self.w_down = din("w_ffn_down", [c.DEPTH, 2, DFF, D])
        self.w_qkv = din("w_qkv_a", [c.NA, D, 3 * D])
        self.w_oa = din("w_o_a", [c.NA, D, D])
        self.w_kv = din("w_kv_shared", [D, 2 * D])
        self.w_qb = din("w_q_b", [c.NBD, D, D])
        self.w_ob = din("w_o_b", [c.NBD, D, D])
        self.outT = nc.dram_tensor("outT", [D, OWN], F32, kind="ExternalOutput").ap()
        self.dbg_on = getattr(self, "dbg_on", False)
        if self.dbg_on:
            self.dbg_d = nc.dram_tensor("dbg", [128, 8192], F32, kind="ExternalOutput").ap()
            self.dbg_col = 0
            self.dbg_map = {}
            self.dbg_toks = []

        self.NX = c.NA + 1
        self.kloc = [nc.dram_tensor(f"kloc{i}", [D, OWN], BF16, kind="Internal").ap() for i in range(self.NX)]
        self.vloc = [nc.dram_tensor(f"vloc{i}", [OWN, D], BF16, kind="Internal").ap() for i in range(self.NX)]
        self.qloc = nc.dram_tensor("qloc", [D, OWN], BF16, kind="Internal").ap()
        self.rows_k = max(128, min(D, (2 << 20) // (OWN * 2)))
        self.rows_v = max(128, min(OWN, (2 << 20) // (D * 2)))
        self.kall = [[nc.dram_tensor(f"kall{i}_{p}", [2 * self.rows_k, OWN], BF16, kind="Internal").ap()
                      for p in range(D // self.rows_k)] for i in range(self.NX)]
        self.vall = [[nc.dram_tensor(f"vall{i}_{p}", [2 * self.rows_v, D], BF16, kind="Internal").ap()
                      for p in range(OWN // self.rows_v)] for i in range(self.NX)]

        with ExitStack() as es:
            self.es = es
            self.P = Prog(nc, es)
            self.alloc()
            self.program()
            block = es.enter_context(nc.Block())
            P = self.P

            @block.tensor
            def _(e):
                P.emit("pe", e)

            @block.scalar
            def _(e):
                P.emit("act", e)

            @block.vector
            def _(e):
                P.emit("dve", e)

            @block.gpsimd
            def _(e):
                P.emit("pool", e)

            @block.sync
            def _(e):
                P.emit("sp", e)
        return nc

    def alloc(self):
        c, nc, P = self.c, self.nc, self.P
        KC, OWN, TT, H, SEQ = c.KC, c.OWN, c.TT, c.H, c.SEQ
        self.off = 16640

        def sb(name, shape, dt, at=None):
            size = int(np.prod(shape[1:])) * (4 if dt == F32 else 2)
            if at is None:
                at = self.off
                self.off = (at + size + 63) // 64 * 64
            t = nc.alloc_sbuf_tensor_at(name, list(shape), dt, offset=at)
            return t, at + size

        self.ones_f, _ = sb("ones_f", [128, 128], F32)
        self.ident_f, _ = sb("ident_f", [128, 128], F32)
        self.ones_b, _ = sb("ones_b", [128, 128], BF16)
        self.ident_b, _ = sb("ident_b", [128, 128], BF16)
        self.m2_f, _ = sb("m2_f", [128, 2, 256], F32)
        self.m2_b, _ = sb("m2_b", [128, 2, 256], BF16)
        self.mb_b, _ = sb("mb_b", [128, 2, 256], BF16)
        self.gv, _ = sb("gv", [128, c.NV * KC], F32)
        self.nT, _ = sb("nT", [128, 4], F32)
        self.one1, _ = sb("one1", [128, 4], F32)
        self.hT, _ = sb("hT", [128, KC, OWN], F32)
        self.xn, _ = sb("xn", [128, KC, OWN], BF16)
        R0 = self.off
        self.R0 = R0
        o = R0
        self.rstd, o = sb("rstd", [128, OWN], F32, at=o)
        self.sq = []
        self.sqb = []
        for i in range(2):
            tb_, _ = sb(f"sqb{i}", [128, OWN], BF16, at=o)
            self.sqb.append(tb_)
            t, o = sb(f"sq{i}", [128, OWN], F32, at=o)
            self.sq.append(t)
        self.actb = []
        act_at = o
        for i in range(4):
            t, o = sb(f"act{i}", [128, 2, OWN], BF16, at=o)
            self.actb.append(t)
        self.sg = []
        sg_at = o
        for i in range(4):
            t, o = sb(f"sg{i}", [128, TT], F32, at=o)
            self.sg.append(t)
        wg, wu, wd = [], [], []
        for i in range(2):
            t, o = sb(f"wg{i}", [128, KC, 256], BF16, at=o)
            wg.append(t)
        for i in range(2):
            t, o = sb(f"wu{i}", [128, KC, 256], BF16, at=o)
            wu.append(t)
        for i in range(4):
            t, o = sb(f"wd{i}", [128, 2, c.D], BF16, at=o)
            wd.append(t)
        self.wg, self.wu, self.wd = Ring(wg, P), Ring(wu, P), Ring(wd, P)
        self.wp = Ring(wg + wu, P)
        ffn_end = o
        self.kst = [nc.alloc_sbuf_tensor_at(f"kst{i}", [128, OWN], BF16, offset=sg_at + i * OWN * 2)
                    for i in range(2)]
        assert 2 * OWN * 2 <= 4 * TT * 4
        self.vst = [nc.alloc_sbuf_tensor_at(f"vst{i}", [128, c.NBL, 256], BF16, offset=act_at + i * 2 * OWN * 2)
                    for i in range(2)]
        assert c.NBL * 256 * 2 <= 2 * OWN * 2
        o = R0
        self.kh, self.vh, self.qh = [], [], []
        for i in range(2):
            t, o = sb(f"kh{i}", [128, SEQ], BF16, at=o)
            self.kh.append(t)
        for i in range(2):
            t, o = sb(f"vh{i}", [128, c.NB, 128], BF16, at=o)
            self.vh.append(t)
        for i in range(2):
            t, o = sb(f"qh{i}", [128, OWN], BF16, at=o)
            self.qh.append(t)
        o_common = o
        self.E, self.SP, self.C, self.A, self.AT = [], [], [], [], []
        for i in range(3):
            t, o = sb(f"E{i}", [128, SEQ], F32, at=o)
            self.E.append(t)
        for i in range(2):
            t, o = sb(f"SP{i}", [128, SEQ + 16], F32, at=o)
            self.SP.append(t)
        for i in range(2):
            t, o = sb(f"C{i}", [128, SEQ], F32, at=o)
            self.C.append(t)
        for i in range(3):
            t, o = sb(f"A{i}", [128, SEQ], BF16, at=o)
            self.A.append(t)
        for i in range(2):
            t, o = sb(f"AT{i}", [128, SEQ], BF16, at=o)
            self.AT.append(t)
        sb_end = o
        o = o_common
        self.bt = []
        for i in range(2):
            t, o = sb(f"bt{i}", [128, 2, 768], F32, at=o)
            self.bt.append(t)
        self.S = []
        for i in range(2):
            t, o = sb(f"S{i}", [128, 768], F32, at=o)
            self.S.append(t)
        self.Pb = []
        for i in range(2):
            t, o = sb(f"Pb{i}", [128, 768], BF16, at=o)
            self.Pb.append(t)
        self.rden = []
        for i in range(2):
            t, o = sb(f"rden{i}", [128, 128], F32, at=o)
            self.rden.append(t)
        band_end = o
        self.sb_total = max(sb_end, band_end, ffn_end)
        assert self.sb_total <= 229344, self.sb_total
        self.psA = self.es.enter_context(nc.psum_tensor("psA", [128, 2048], F32))
        self.psB = self.es.enter_context(nc.psum_tensor("psB", [128, 2048], F32))
        self.pfree = {}
        self.c_pe = Counter(P, 1)
        self.c_act = Counter(P, 1)
        self.c_dve = Counter(P, 1)
        self.c_w = Counter(P, 16)
        self.c_ld = Counter(P, 16)
        self.c_st = Counter(P, 16)
        self.c_dbg = Counter(P, 16)
        self.c_cc = Counter(P, 1)
        self.c_pool = Counter(P, 1)
        self.c_qh = [Counter(P, 16) for _ in range(2)]
        self.c_kh = [Counter(P, 16) for _ in range(2)]
        self.c_vh = [Counter(P, 16) for _ in range(2)]
        self.c_bt = [Counter(P, 16) for _ in range(2)]
        self.c_kst = [Counter(P, 16) for _ in range(2)]
        self.c_vst = [Counter(P, 16) for _ in range(2)]

    def bankA(self, i, n=None):
        n = self.c.TT if n is None else n
        return self.psA[:, i * 512:i * 512 + n]

    def bankB(self, i, n=None):
        n = self.c.TT if n is None else n
        return self.psB[:, i * 512:i * 512 + n]

    def program(self):
        c, P = self.c, self.P
        KC, OWN = c.KC, c.OWN
        toks = []
        cst = self.cst_d
        toks.append(P.op("sp", lambda e: e.dma_start(out=self.ones_f[:], in_=cst[:, 0:128]), cnt=self.c_ld))
        toks.append(P.op("sp", lambda e: e.dma_start(out=self.ident_f[:], in_=cst[:, 256:384]), cnt=self.c_ld))
        toks.append(P.op("sp", lambda e: e.dma_start(out=self.gv[:], in_=self.gv_d), cnt=self.c_ld))
        toks.append(P.op("sp", lambda e: e.dma_start(out=self.m2_f[:], in_=self.m2_d.rearrange("p (a b) -> p a b", a=2)), cnt=self.c_ld))
        toks.append(P.op("pool", lambda e: e.dma_start(out=self.ones_b[:], in_=cst[:, 0:128]), cnt=self.c_w))
        toks.append(P.op("pool", lambda e: e.dma_start(out=self.ident_b[:], in_=cst[:, 256:384]), cnt=self.c_w))
        toks.append(P.op("pool", lambda e: e.dma_start(out=self.m2_b[:], in_=self.m2_d.rearrange("p (a b) -> p a b", a=2)), cnt=self.c_w))
        toks.append(P.op("pool", lambda e: e.dma_start(out=self.mb_b[:], in_=self.mb_d.rearrange("p (a b) -> p a b", a=2)), cnt=self.c_w))
        t1 = P.op("dve", lambda e: e.memset(self.one1[:], 1.0), cnt=self.c_dve)
        self.const_tok = toks + [t1]
        xv = self.xT.rearrange("(c p) t -> p c t", p=128)
        self.h_tok = []
        self.c_x = [Counter(P, 16) for _ in range(KC)]
        hk = {}
        for g in range(KC):
            q = "sp" if g % 2 == 0 else "act"
            t = P.op(q, lambda e, g=g: e.dma_start(out=self.hT[:, g, :], in_=xv[:, g, :]), cnt=self.c_x[g])
            self.h_tok.append(t)
            hk[g] = [t] + (self.const_tok if g == 0 else [])
        self.h_tok_k = hk
        self.x_tok0 = list(self.h_tok)
        self.h_tok += self.const_tok
        self.xn_free = []
        self.rstd_free = None
        self.sq_free = [None, None]
        self.attn_done = []
        self.qT_free = []

        nst = getattr(self, "nstages", 99)
        si = 0
        for l in range(c.DEPTH):
            if si >= nst:
                break
            self.ffn(l, 0)
            si += 1
            if si >= nst:
                break
            if l < c.NA:
                self.rmsnorm(c.DEPTH * 2 + l)
                self.x_tok = []
                self.proj_k(self.w_qkv[l][:, c.D:2 * c.D], l)
                self.proj_v(self.w_qkv[l][:, 2 * c.D:3 * c.D], l, after_prefetch=lambda l=l: self.exchange(l, "k"))
                self.proj_q(self.w_qkv[l][:, 0:c.D], after_prefetch=lambda l=l: self.exchange(l, "v"))
                self.attn_sb(l)
                self.oproj(self.w_oa[l])
            else:
                lb = l - c.NA
                if lb == 0:
                    self.rmsnorm(3 * c.DEPTH)
                    self.x_tok = []
                    self.proj_k(self.w_kv[:, 0:c.D], c.NA)
                    self.proj_v(self.w_kv[:, c.D:2 * c.D], c.NA, after_prefetch=lambda: self.exchange(c.NA, "k"))
                    self.pending_xv = True
                self.rmsnorm(c.DEPTH * 2 + l)
                if getattr(self, "pending_xv", False):
                    self.pending_xv = False
                    self.proj_q(self.w_qb[lb], after_prefetch=lambda: self.exchange(c.NA, "v"))
                else:
                    self.proj_q(self.w_qb[lb])
                self.attn_band(lb)
                self.oproj(self.w_ob[lb])
            si += 1
            if si >= nst:
                break
            self.ffn(l, 1)
            si += 1
        if si >= nst and nst < 3 * c.DEPTH + 1:
            self.dump_h()
        else:
            self.final_norm(3 * c.DEPTH + 1)

    def dbg(self, name, ap, n, waits):
        if not self.dbg_on:
            return
        c0 = self.dbg_col
        self.dbg_col += n
        self.dbg_map[name] = (c0, n)
        t = self.P.op("pool", lambda e: e.dma_start(out=self.dbg_d[:, c0:c0 + n], in_=ap), waits=waits, cnt=self.c_dbg)
        self.dbg_toks.append(t)
        return t

    def dump_h(self):
        P = self.P
        ov = self.outT.rearrange("(c p) t -> p c t", p=128)
        t_o = P.op("sp", lambda e: e.dma_start(out=ov, in_=self.hT[:]), waits=[self.h_tok], cnt=self.c_st)
        for name in Prog.ENG:
            P.op(name, lambda e: e.wait_ge(t_o[0], t_o[1]))
        if self.dbg_on:
            for t in self.dbg_toks:
                P.op("sp", lambda e, t=t: e.wait_ge(t[0], t[1]))

    def rings_to_proj(self):
        if getattr(self, "ring_mode", "ffn") == "proj":
            return
        self.ring_mode = "proj"
        self.wp.free = [self.wg.free[0], self.wg.free[1], self.wu.free[0], self.wu.free[1]]

    def rings_to_ffn(self):
        if getattr(self, "ring_mode", "ffn") == "ffn":
            return
        self.ring_mode = "ffn"
        allp = [t for t in self.wp.free]
        self.wg.free = [[self.wg.free[0], allp[0]], [self.wg.free[1], allp[1]]]
        self.wu.free = [[self.wu.free[0], allp[2]], [self.wu.free[1], allp[3]]]

    def load_w(self, ring, src, extra=()):
        P = self.P
        s = ring.take()
        dst = ring.slots[s]
        tok = P.op("pool", lambda e: e.dma_start(out=dst[:], in_=src), waits=[ring.free[s]] + list(extra), cnt=ring.cnt[s])
        return s, tok

    def wtile(self, W, t):
        return W.rearrange("(c p) f -> p c f", p=128)[:, :, t * 256:(t + 1) * 256]

    def rmsnorm(self, gi, to_out=False):
        c, P = self.c, self.P
        KC, TT, NTT, D = c.KC, c.TT, c.NTT, c.D
        hT, xn, rstd = self.hT, self.xn, self.rstd
        hk = getattr(self, "h_tok_k", None)
        t_mm = None
        for k in range(KC):
            sqb = self.sqb[k % 2]
            wk = hk[k] if hk is not None else (self.h_tok if k == 0 else None)
            t_sq = P.op("act", lambda e, sqb=sqb, k=k: e.activation(out=sqb[:], in_=hT[:, k, :], func=AF.Square),
                        waits=[wk, self.sq_free[k % 2]], cnt=self.c_act)
            for tt in range(NTT):
                last = tt == NTT - 1
                w = [t_sq]
                if k == 0:
                    w.append(self.pfree.get(("B", tt)))
                t = P.op("pe", lambda e, sqb=sqb, tt=tt, k=k: e.matmul(
                    self.bankB(tt), self.ones_b[:], sqb[:, tt * TT:(tt + 1) * TT], start=(k == 0), stop=(k == KC - 1)),
                    waits=w, cnt=self.c_pe if last else None)
                if last:
                    t_mm = t
            self.sq_free[k % 2] = t_mm
        t_r = None
        for tt in range(NTT):
            t_r = P.op("act", lambda e, tt=tt: e.activation(out=rstd[:, tt * TT:(tt + 1) * TT], in_=self.bankB(tt),
                                                            func=AF.Ln, scale=1.0 / D, bias=EPS),
                       waits=[t_mm, self.rstd_free], cnt=self.c_act)
            self.pfree[("B", tt)] = t_r
        t_rs = P.op("act", lambda e: e.activation(out=rstd[:], in_=rstd[:], func=AF.Exp, scale=-0.5), cnt=self.c_act)
        KD = KC
        dst = hT if to_out else xn
        toks = {}
        first = {"dve": True, "pool": True}
        for k in range(KC):
            eng = "dve" if k < KD else "pool"
            fn = lambda e, k=k: e.scalar_tensor_tensor(out=dst[:, k, :], in0=hT[:, k, :],
                                                       scalar=self.gv[:, gi * KC + k:gi * KC + k + 1],
                                                       in1=rstd[:], op0=ALU.mult, op1=ALU.mult)
            w = [t_rs, self.h_tok, self.xn_free] if first[eng] else []
            first[eng] = False
            toks[k] = P.op(eng, fn, waits=w, cnt=self.c_dve if eng == "dve" else self.c_pool)
        last_d = toks[KD - 1]
        last_p = toks[KC - 1] if KD < KC else None
        self.rstd_free = [last_d, last_p]
        self.xn_ready = [last_d, last_p]
        self.xn_ready_k = toks
        self.xn_free = []
        self.h_tok_k = None
        return [last_d, last_p]

    def final_norm(self, gi):
        P = self.P
        self.rmsnorm(gi, to_out=True)
        ov = self.outT.rearrange("(c p) t -> p c t", p=128)
        toks = self.xn_ready_k
        outs = []
        for k in range(self.c.KC):
            outs.append(P.op("sp", lambda e, k=k: e.dma_start(out=ov[:, k, :], in_=self.hT[:, k, :]), waits=[toks[k]], cnt=self.c_st))
        t_o = outs[-1]
        for name in Prog.ENG:
            P.op(name, lambda e: e.wait_ge(t_o[0], t_o[1]))
        if self.dbg_on:
            for t in self.dbg_toks:
                P.op("sp", lambda e, t=t: e.wait_ge(t[0], t[1]))

    def ffn(self, l, j):
        c, P = self.c, self.P
        KC, TT, NTT, NG, OWN = c.KC, c.TT, c.NTT, c.NG, c.OWN
        hT, xn = self.hT, self.xn
        self.rmsnorm(l * 2 + j)
        self.rings_to_ffn()
        Wg, Wu, Wd = self.w_gate[l, j], self.w_up[l, j], self.w_down[l, j]
        wdv = Wd.rearrange("(c p) d -> p c d", p=128)
        extra_first = list(self.attn_done)
        loads = {}

        loads_d = {}

        def issue(g):
            ex = extra_first if g < 2 else []
            sg_, tg = self.load_w(self.wg, self.wtile(Wg, g), ex)
            su_, tu = self.load_w(self.wu, self.wtile(Wu, g), ex)
            loads[g] = (sg_, tg, su_, tu)

        def issue_d(g):
            ex = extra_first if g < 2 else []
            if l == 0 and j == 0 and g == 0:
                ex = ex + list(self.x_tok0)
            loads_d[g] = self.load_w(self.wd, wdv[:, 2 * g:2 * g + 2, :], ex + list(self.qT_free))

        issue(0)
        if NG > 1:
            issue(1)
        for g_ in range(min(4, NG)):
            issue_d(g_)
        act_ready = {}
        act_free = getattr(self, "act_free", [None] * 4)
        sg_free = getattr(self, "sg_free", [None] * 4)
        sgi = 0
        h_new = []
        dn_last = {}

        def down(gs):
            t_pe = None
            nmm = 2 * len(gs)
            for dc in range(KC):
                st = dc % 2
                mi = 0
                for g in gs:
                    sd_, td = loads_d[g]
                    wdt = self.wd.slots[sd_]
                    ab = self.actb[g % 4]
                    for fl in range(2):
                        for tt in range(NTT):
                            bi = st * 2 + tt if NTT == 2 else st
                            w = []
                            if mi == 0:
                                w.append(self.pfree.get(("B", bi)))
                            if dc == 0 and fl == 0 and tt == 0:
                                w += [td] + act_ready[g]
                            last = (mi == nmm - 1 and tt == NTT - 1)
                            t = P.op("pe", lambda e, bi=bi, fl=fl, dc=dc, tt=tt, wdt=wdt, ab=ab, mi=mi: e.matmul(
                                self.bankB(bi), wdt[:, fl, dc * 128:(dc + 1) * 128], ab[:, fl, tt * TT:(tt + 1) * TT],
                                start=(mi == 0), stop=(mi == nmm - 1)), waits=w, cnt=self.c_pe if last else None)
                            if last:
                                t_pe = t
                        mi += 1
                for tt in range(NTT):
                    bi = st * 2 + tt if NTT == 2 else st
                    t_acc = P.op("dve", lambda e, bi=bi, dc=dc, tt=tt: e.scalar_tensor_tensor(
                        out=hT[:, dc, tt * TT:(tt + 1) * TT], in0=self.bankB(bi), scalar=0.5,
                        in1=hT[:, dc, tt * TT:(tt + 1) * TT], op0=ALU.mult, op1=ALU.add),
                        waits=[t_pe], cnt=self.c_dve)
                    self.pfree[("B", bi)] = t_acc
                    dn_last[(dc, tt)] = t_acc
            for g in gs:
                self.wd.free[loads_d[g][0]] = t_pe
                act_free[g % 4] = t_pe
            for g in gs:
                if g + 4 < NG:
                    issue_d(g + 4)

        for g in range(NG):
            sg_, tg, su_, tu = loads[g]
            wgt, wut = self.wg.slots[sg_], self.wu.slots[su_]
            act_ready[g] = []
            for fl in range(2):
                t_g = None
                for k in range(KC):
                    for tt in range(NTT):
                        w = []
                        if k == 0:
                            w.append(self.pfree.get(("A", tt)))
                            if tt == 0 and fl == 0:
                                w.append(tg)
                        if g == 0 and fl == 0 and tt == 0:
                            w.append(self.xn_ready_k[k] if self.xn_ready_k is not None else (self.xn_ready if k == 0 else None))
                        last = (k == KC - 1 and tt == NTT - 1)
                        t = P.op("pe", lambda e, k=k, tt=tt, fl=fl, wgt=wgt: e.matmul(
                            self.bankA(tt), wgt[:, k, fl * 128:(fl + 1) * 128], xn[:, k, tt * TT:(tt + 1) * TT],
                            start=(k == 0), stop=(k == KC - 1)), waits=w, cnt=self.c_pe if last else None)
                        if last:
                            t_g = t
                t_u = None
                for k in range(KC):
                    for tt in range(NTT):
                        w = []
                        if k == 0:
                            w.append(self.pfree.get(("A", 2 + tt)))
                            if tt == 0 and fl == 0:
                                w.append(tu)
                        last = (k == KC - 1 and tt == NTT - 1)
                        t = P.op("pe", lambda e, k=k, tt=tt, fl=fl, wut=wut: e.matmul(
                            self.bankA(2 + tt), wut[:, k, fl * 128:(fl + 1) * 128], xn[:, k, tt * TT:(tt + 1) * TT],
                            start=(k == 0), stop=(k == KC - 1)), waits=w, cnt=self.c_pe if last else None)
                        if last:
                            t_u = t
                for tt in range(NTT):
                    sgt = self.sg[sgi % 4]
                    t_s = P.op("act", lambda e, sgt=sgt, tt=tt: e.activation(out=sgt[:], in_=self.bankA(tt), func=AF.Silu),
                               waits=[t_g, sg_free[sgi % 4]], cnt=self.c_act)
                    self.pfree[("A", tt)] = t_s
                    ab = self.actb[g % 4]
                    t_m = P.op("dve", lambda e, sgt=sgt, tt=tt, fl=fl, ab=ab: e.tensor_tensor(
                        out=ab[:, fl, tt * TT:(tt + 1) * TT], in0=self.bankA(2 + tt), in1=sgt[:], op=ALU.mult),
                        waits=[t_u, t_s, act_free[g % 4]], cnt=self.c_dve)
                    self.pfree[("A", 2 + tt)] = t_m
                    sg_free[sgi % 4] = t_m
                    sgi += 1
                    act_ready[g].append(t_m)
                if fl == 1:
                    self.wg.free[sg_] = t_g
                    self.wu.free[su_] = t_u
            if g + 2 < NG:
                issue(g + 2)
            if g >= 2 and g % 2 == 0:
                down([g - 2, g - 1])
        if NG % 2 == 0:
            down([NG - 2, NG - 1])
        else:
            down([NG - 1])
        self.act_free = act_free
        self.sg_free = sg_free
        self.xn_free = [self.wg.free[loads[NG - 1][0]], self.wu.free[loads[NG - 1][2]]]
        self.h_tok = [dn_last[(dc, tt)] for dc in range(KC) for tt in range(NTT)]
        self.h_tok_k = {dc: [dn_last[(dc, tt)] for tt in range(NTT)] for dc in range(KC)}
        self.attn_done = []
        self.qT_free = []

    def proj_fm(self, W, evac, extra_first=(), after_prefetch=None):
        c, P = self.c, self.P
        KC, TT, NTT = c.KC, c.TT, c.NTT
        ntile = W.shape[1] // 256
        loads = {}

        self.rings_to_proj()
        PD = 4

        def issue(t):
            loads[t] = self.load_w(self.wp, self.wtile(W, t), list(extra_first) if t < PD else [])

        for t_ in range(min(PD, ntile)):
            issue(t_)
        if after_prefetch is not None:
            after_prefetch()
        t_last = None
        for t in range(ntile):
            s, tl = loads[t]
            wt = self.wp.slots[s]
            for fl in range(2):
                oc = 2 * t + fl
                st = oc % 2
                t_g = None
                for k in range(KC):
                    for tt in range(NTT):
                        bi = st * 2 + tt
                        w = []
                        if k == 0:
                            w.append(self.pfree.get(("A", bi)))
                            if tt == 0 and fl == 0:
                                w.append(tl)
                        if t == 0 and fl == 0 and tt == 0:
                            w.append(self.xn_ready_k[k] if self.xn_ready_k is not None else (self.xn_ready if k == 0 else None))
                        last = (k == KC - 1 and tt == NTT - 1)
                        tk = P.op("pe", lambda e, k=k, tt=tt, fl=fl, bi=bi, wt=wt: e.matmul(
                            self.bankA(bi), wt[:, k, fl * 128:(fl + 1) * 128], self.xn[:, k, tt * TT:(tt + 1) * TT],
                            start=(k == 0), stop=(k == KC - 1)), waits=w, cnt=self.c_pe if last else None)
                        if last:
                            t_g = tk
                for tt in range(NTT):
                    bi = st * 2 + tt
                    self.pfree[("A", bi)] = evac(oc, tt, self.bankA(bi), t_g)
                t_last = t_g
            self.wp.free[s] = t_last
            if t + PD < ntile:
                issue(t + PD)
        self.xn_free = [t_last]
        return t_last

    def proj_dram(self, W, dst, toks, after_prefetch=None):
        c, P = self.c, self.P
        TT, NTT = c.TT, c.NTT
        kst_free = getattr(self, "kst_free", [None, None])
        state = {}

        def evac(oc, tt, bank, tok):
            sl = oc % 2
            d = self.kst[sl][:, tt * TT:(tt + 1) * TT]
            w = [tok, kst_free[sl]]
            if (oc + tt) % 2 == 0:
                t = P.op("act", lambda e: e.activation(out=d, in_=bank, func=AF.Copy), waits=w, cnt=self.c_act)
            else:
                t = P.op("dve", lambda e: e.tensor_copy(out=d, in_=bank), waits=w, cnt=self.c_dve)
            state.setdefault(oc, []).append(t)
            if tt == NTT - 1:
                ts = P.op("sp", lambda e: e.dma_start(out=dst[oc * 128:(oc + 1) * 128, :], in_=self.kst[sl][:]),
                          waits=state[oc], cnt=self.c_kst[sl])
                kst_free[sl] = ts
                toks.append(ts)
            return t

        self.proj_fm(W, evac, after_prefetch=after_prefetch)
        self.kst_free = kst_free

    def proj_q(self, W, after_prefetch=None):
        self.q_st_tok = []
        self.proj_dram(W, self.qloc, self.q_st_tok, after_prefetch=after_prefetch)

    def proj_k(self, W, xi):
        self.k_st_tok = []
        self.proj_dram(W, self.kloc[xi], self.k_st_tok)

    def proj_v(self, W, xi, after_prefetch=None):
        c, P = self.c, self.P
        KC, NBL = c.KC, c.NBL
        ntile = W.shape[1] // 256
        loads = {}
        vst_free = getattr(self, "vst_free", [None, None])
        self.v_st_tok = []

        self.rings_to_proj()
        PD = 4

        def issue(t):
            loads[t] = self.load_w(self.wp, self.wtile(W, t))

        for t_ in range(min(PD, ntile)):
            issue(t_)
        if after_prefetch is not None:
            after_prefetch()
        t_g = None
        vv = self.vloc[xi].rearrange("(tb p) d -> p tb d", p=128)
        for t in range(ntile):
            s, tl = loads[t]
            wt = self.wp.slots[s]
            sl = t % 2
            evs = []
            for tb in range(NBL):
                bi = tb % 4
                for k in range(KC):
                    w = []
                    if k == 0:
                        w.append(self.pfree.get(("B", bi)))
                        if tb == 0:
                            w.append(tl)
                    if t == 0 and tb == 0:
                        w.append(self.xn_ready_k[k] if self.xn_ready_k is not None else (self.xn_ready if k == 0 else None))
                    last = k == KC - 1
                    tk = P.op("pe", lambda e, k=k, tb=tb, bi=bi, wt=wt: e.matmul(
                        self.bankB(bi, 256), self.xn[:, k, tb * 128:(tb + 1) * 128], wt[:, k, :],
                        start=(k == 0), stop=(k == KC - 1)), waits=w, cnt=self.c_pe if last else None)
                    if last:
                        t_g = tk
                dst = self.vst[sl][:, tb, :]
                w = [t_g, vst_free[sl]]
                if tb % 2 == 0:
                    te = P.op("act", lambda e, dst=dst, bi=bi: e.activation(out=dst, in_=self.bankB(bi, 256), func=AF.Copy),
                              waits=w, cnt=self.c_act)
                else:
                    te = P.op("dve", lambda e, dst=dst, bi=bi: e.tensor_copy(out=dst, in_=self.bankB(bi, 256)),
                              waits=w, cnt=self.c_dve)
                self.pfree[("B", bi)] = te
                evs.append(te)
            ts = P.op("sp", lambda e, sl=sl, t=t: e.dma_start(out=vv[:, :, t * 256:(t + 1) * 256], in_=self.vst[sl][:]),
                      waits=evs, cnt=self.c_vst[sl])
            vst_free[sl] = ts
            self.v_st_tok.append(ts)
            self.wp.free[s] = t_g
            if t + PD < ntile:
                issue(t + PD)
        self.vst_free = vst_free
        self.xn_free = [t_g]

    def exchange(self, xi, kind):
        c, P = self.c, self.P
        if kind == "k":
            w0, loc, pieces, rows = self.k_st_tok, self.kloc[xi], self.kall[xi], self.rows_k
        else:
            w0, loc, pieces, rows = self.v_st_tok, self.vloc[xi], self.vall[xi], self.rows_v
        for p, dst in enumerate(pieces):
            src = loc[p * rows:(p + 1) * rows, :]
            t = P.op("pool", lambda e, src=src, dst=dst: e.collective_compute(
                "AllGather", ALU.bypass, replica_groups=PAIRS, ins=[src.opt()], outs=[dst.opt()]),
                waits=w0 if p == 0 else [], cnt=self.c_cc)
            self.x_tok.append(t)

    def load_kq(self, xi, h, slot, extra):
        c, P = self.c, self.P
        rk = self.rows_k
        pk = (h * 128) // rk
        r0 = h * 128 - pk * rk
        src = self.kall[xi][pk].rearrange("(r x) (m q i) -> x r m q i", r=2, q=2, i=128)
        dst = self.kh[slot][:].rearrange("p (m g i) -> p m g i", g=4, i=128)
        w = self.x_tok + list(extra)
        tk = []
        for r in range(2):
            for par in range(2):
                g = r if par == 0 else 3 - r
                tk.append(P.op("sp", lambda e, r=r, par=par, g=g: e.dma_start(
                    out=dst[:, :, g, :], in_=src[r0:r0 + 128, r, :, par, :]),
                    waits=w + [self.kh_free[slot]], cnt=self.c_kh[slot]))
        tq = [P.op("sp", lambda e: e.dma_start(out=self.qh[slot][:], in_=self.qloc[h * 128:(h + 1) * 128, :]),
                   waits=self.q_st_tok + list(extra) + [self.qh_free[slot]], cnt=self.c_qh[slot])]
        return tk, tq

    def load_v(self, xi, h, slot, extra):
        c, P = self.c, self.P
        rv = self.rows_v
        w = self.x_tok + list(extra)
        tv = []
        bpp = rv // 128
        vhv = self.vh[slot][:].rearrange("p (r b) d -> p r b d", r=2)
        for pv, piece in enumerate(self.vall[xi]):
            src = piece.rearrange("(r b p) d -> p r b d", r=2, p=128)
            for r in range(2):
                tv.append(P.op("sp", lambda e, src=src, r=r, pv=pv: e.dma_start(
                    out=vhv[:, r, pv * bpp:(pv + 1) * bpp, :], in_=src[:, r, :, h * 128:(h + 1) * 128]),
                    waits=w + [self.vh_free[slot]], cnt=self.c_vh[slot]))
        return tv

    def attn_sb(self, l):
        c, P = self.c, self.P
        H, NBL, SEQ = c.H, c.NBL, c.SEQ
        scale = 128.0 ** -0.5
        psZ = self.psA
        psT = self.psB[:, 0:1024].bitcast(BF16)
        self.kh_free = [None, None]
        self.vh_free = [None, None]
        self.qh_free = [None, None]
        phase_w = list(self.xn_free) + list(self.k_st_tok) + list(self.v_st_tok) + list(self.q_st_tok)
        t0 = [P.op("dve", lambda e, i=i: e.memset(self.SP[i][:, 0:1], 0.0), waits=phase_w, cnt=self.c_dve) for i in range(2)]
        lb_order = []
        for q_ in range(NBL // 2):
            lb_order += [q_, NBL - 1 - q_]
        if NBL % 2:
            lb_order.append(NBL // 2)
        iters = [(h, lb) for h in range(H) for lb in lb_order]
        first_lb, last_lb = lb_order[0], lb_order[-1]
        lbv = lb_order[min(5, NBL - 1)]
        N = len(iters)
        kq, vv = {}, {}
        kq[0] = self.load_kq(l, 0, 0, phase_w)
        vv[0] = self.load_v(l, 0, 0, phase_w)
        e_free = [None, None, None]
        sp_free = [t0[0], t0[1]]
        c_free = [None, None]
        a_free = [None, None, None]
        at_free = [None, None]
        T = {}
        o_tok = []
        t_z_last = None
        for s in range(N + 4):
            i2 = s - 2
            if 0 <= i2 < N:
                d = T[i2]
                kl, ni = d["kl"], d["ni"]
                Eb, Cb, ab = self.E[i2 % 3], self.C[i2 % 2], self.A[i2 % 3]
                t_g = P.op("act", lambda e, kl=kl, ni=ni, Cb=Cb: e.activation(out=Cb[:, 0:kl], in_=Cb[:, 0:kl], func=AF.Exp,
                                                                               bias=self.nT[:, ni:ni + 1]), waits=[d["t_nt"]], cnt=self.c_act)
                kp = 0
                d["kp"] = kp
                t_ap = None
                d["t_g"] = t_g
                d["t_ap"] = t_ap
            if s < N:
                h, lb = iters[s]
                if lb == first_lb and h + 1 < H:
                    kq[h + 1] = self.load_kq(l, h + 1, (h + 1) % 2, phase_w)
                tk, tq = kq[h]
                kh, qh = self.kh[h % 2], self.qh[h % 2]
                kl = (2 * lb + 2) * 128
                b = s % 2
                t_z = None
                segs = []
                c0 = 0
                while c0 < kl - 256:
                    n = min(512, kl - 256 - c0)
                    segs.append((c0, n, True, True, None))
                    c0 += n
                segs.append((kl - 256, 256, True, False, None))
                segs.append((kl - 256, 256, False, True, lb % 2))
                for si, (c0, n, st_, sp_, mk) in enumerate(segs):
                    w = (tk + tq + [self.pfree.get("Z")]) if si == 0 else []
                    last = si == len(segs) - 1
                    if mk is None:
                        fn = lambda e, c0=c0, n=n, qh=qh, kh=kh, lb=lb, st_=st_, sp_=sp_: e.matmul(
                            psZ[:, c0:c0 + n], qh[:, lb * 128:(lb + 1) * 128], kh[:, c0:c0 + n], start=st_, stop=sp_)
                    else:
                        fn = lambda e, c0=c0, n=n, mk=mk: e.matmul(
                            psZ[:, c0:c0 + n], self.ident_b[:], self.mb_b[:, mk, :], start=False, stop=True)
                    t = P.op("pe", fn, waits=w, cnt=self.c_pe if last else None)
                    if last:
                        t_z = t
                if lb == last_lb:
                    self.kh_free[h % 2] = t_z
                    self.qh_free[h % 2] = t_z
                t_z_last = t_z
                Eb, SPb = self.E[s % 3], self.SP[b]
                t_e = P.op("act", lambda e, kl=kl, Eb=Eb: e.activation(out=Eb[:, 0:kl], in_=psZ[:, 0:kl], func=AF.Exp, scale=scale),
                           waits=[t_z, e_free[s % 3]], cnt=self.c_act)
                self.pfree["Z"] = t_e
                t_l = P.op("act", lambda e, kl=kl, Eb=Eb, SPb=SPb: e.activation(out=SPb[:, 1:kl + 1], in_=Eb[:, 0:kl], func=AF.Ln, bias=1.0),
                           waits=[sp_free[b]], cnt=self.c_act)
                T[s] = dict(h=h, lb=lb, kl=kl, t_l=t_l)
            j4 = s - 4
            if 0 <= j4 < N:
                d = T[j4]
                h, lb, kl = d["h"], d["lb"], d["kl"]
                nkb = 2 * lb + 2
                b = j4 % 2
                atb = self.AT[b]
                vh = self.vh[h % 2]
                tv = vv[h]
                t_cp = d["t_cp"]
                ob = 2 + b
                t_av = None
                for gb in range(nkb):
                    pos = own(gb) * NBL + gb // 2
                    last = gb == nkb - 1
                    w = [t_cp, self.pfree.get(("B", ob))] + tv if gb == 0 else []
                    t = P.op("pe", lambda e, gb=gb, pos=pos, vh=vh, atb=atb, ob=ob, nkb=nkb: e.matmul(
                        self.bankB(ob, 128), vh[:, pos, :], atb[:, gb * 128:(gb + 1) * 128], start=(gb == 0), stop=(gb == nkb - 1)),
                        waits=w, cnt=self.c_pe if last else None)
                    if last:
                        t_av = t
                at_free[b] = t_av
                if lb == last_lb:
                    self.vh_free[h % 2] = t_av
                t_o = P.op("act", lambda e, ob=ob, h=h, lb=lb: e.activation(out=self.xn[:, h, lb * 128:(lb + 1) * 128],
                                                                            in_=self.bankB(ob, 128), func=AF.Copy),
                           waits=[t_av], cnt=self.c_act)
                self.pfree[("B", ob)] = t_o
                o_tok.append(t_o)
            j = s - 3
            if 0 <= j < N:
                d = T[j]
                h, lb, kl = d["h"], d["lb"], d["kl"]
                nkb = 2 * lb + 2
                b = j % 2
                ab, atb = self.A[j % 3], self.AT[b]
                vh = self.vh[h % 2]
                tv = vv[h]
                t_t = None
                for kb in range(nkb):
                    last = kb == nkb - 1
                    t = P.op("pe", lambda e, kb=kb, ab=ab: e.transpose(psT[:, kb * 128:(kb + 1) * 128], ab[:, kb * 128:(kb + 1) * 128],
                                                                       self.ident_b[:]),
                             waits=[d["t_ma"], self.pfree.get("T")] if kb == 0 else [], cnt=self.c_pe if last else None)
                    if last:
                        t_t = t
                a_free[j % 3] = t_t
                d["t_t"] = t_t
            i = s - 1
            if 0 <= i < N:
                d = T[i]
                h, lb, kl = d["h"], d["lb"], d["kl"]
                b = i % 2
                Eb, SPb, Cb, ab = self.E[i % 3], self.SP[b], self.C[b], self.A[i % 3]
                t_sc = P.op("dve", lambda e, kl=kl, SPb=SPb, Cb=Cb: e.tensor_tensor_scan(
                    out=Cb[:, 0:kl], data0=self.one1[:, 0:1].to_broadcast([128, kl]), data1=SPb[:, 0:kl], initial=0.0,
                    op0=ALU.mult, op1=ALU.add), waits=[d["t_l"], c_free[b]], cnt=self.c_dve)
                sp_free[b] = t_sc
                ni = i % 4
                t_nt = P.op("dve", lambda e, kl=kl, ni=ni, Cb=Cb: e.tensor_scalar(out=self.nT[:, ni:ni + 1], in0=Cb[:, kl - 1:kl], scalar1=-1.0,
                                                                                  scalar2=None, op0=ALU.mult), waits=[t_sc], cnt=self.c_dve)
                d["t_nt"] = t_nt
                d["ni"] = ni
            if 0 <= i2 < N:
                d = T[i2]
                kl, kp = d["kl"], d["kp"]
                Eb, Cb, ab = self.E[i2 % 3], self.C[i2 % 2], self.A[i2 % 3]
                t_ad = P.op("dve", lambda e, kl=kl, kp=kp, Eb=Eb, Cb=Cb, ab=ab: e.tensor_tensor(out=ab[:, kp:kl], in0=Eb[:, kp:kl], in1=Cb[:, kp:kl], op=ALU.mult),
                            waits=[d["t_g"], a_free[i2 % 3]], cnt=self.c_dve)
                t_a = [t_ad]
                e_free[i2 % 3] = t_a
                c_free[i2 % 2] = t_a
                d["t_ma"] = t_a
            if 0 <= j < N:
                d = T[j]
                h, lb, kl = d["h"], d["lb"], d["kl"]
                nkb = 2 * lb + 2
                b = j % 2
                ab, atb = self.A[j % 3], self.AT[b]
                vh = self.vh[h % 2]
                tv = vv[h]
                t_t = d["t_t"]
                if j % 3 == 2:
                    t_cp = P.op("dve", lambda e, kl=kl, atb=atb: e.tensor_copy(out=atb[:, 0:kl], in_=psT[:, 0:kl]),
                                waits=[t_t, at_free[b]], cnt=self.c_dve)
                else:
                    t_cp = P.op("act", lambda e, kl=kl, atb=atb: e.activation(out=atb[:, 0:kl], in_=psT[:, 0:kl], func=AF.Copy),
                                waits=[t_t, at_free[b]], cnt=self.c_act)
                self.pfree["T"] = t_cp
                d["t_cp"] = t_cp
            if s < N:
                h_, lb_ = iters[s]
                if lb_ == lbv and h_ + 1 < H:
                    vv[h_ + 1] = self.load_v(l, h_ + 1, (h_ + 1) % 2, phase_w)
        self.attn_done = [o_tok[-1], self.pfree["T"], e_free[0], e_free[1], e_free[2], T[N - 1]["t_ma"], T[N - 2]["t_ma"]]
        self.qT_free = []
        self.xn_ready = [o_tok[-1], o_tok[-2]]
        self.xn_ready_k = None

    def attn_band(self, lbi):
        c, P = self.c, self.P
        H, NBL = c.H, c.NBL
        xi = c.NA
        scale = 128.0 ** -0.5
        if lbi == 0:
            self.kh_free = [None, None]
            self.vh_free = [None, None]
        self.qh_free = [None, None]
        phase_w = list(self.xn_free) + list(self.q_st_tok)
        if lbi == 0:
            phase_w += list(self.k_st_tok) + list(self.v_st_tok)
        bt_free = [None, None]
        s_free = [None, None]
        p_free = [None, None]
        rd_free = [None, None]

        def load(h, slot):
            tk, tq = self.load_kq(xi, h, slot, phase_w)
            tb = P.op("sp", lambda e: e.dma_start(out=self.bt[slot][:], in_=self.bt_d[lbi, h].rearrange("p (a f) -> p a f", a=2)),
                      waits=[bt_free[slot]] + phase_w, cnt=self.c_bt[slot])
            return tk + tq, tb

        iters = [(h, lb) for h in range(H) for lb in range(NBL)]
        N = len(iters)
        heads = {0: load(0, 0)}
        vv = {0: self.load_v(xi, 0, 0, phase_w)}
        T = {}
        o_tok = []
        lbv = min(2, NBL - 1)
        for s in range(N + 1):
            if s < N:
                h, lb = iters[s]
                if lb == 0 and h + 1 < H:
                    heads[h + 1] = load(h + 1, (h + 1) % 2)
                if lb == lbv and h + 1 < H:
                    vv[h + 1] = self.load_v(xi, h + 1, (h + 1) % 2, phase_w)
                tk, tb = heads[h]
                kh, bt = self.kh[h % 2], self.bt[h % 2]
                sl = s % 2
                j0 = max(0, 4 - 2 * lb)
                qs = self.qh[h % 2][:, lb * 128:(lb + 1) * 128]
                psS = self.psA[:, sl * 1024:sl * 1024 + 768]
                t_s = None
                for j in range(j0, 6):
                    kb = 2 * lb - 4 + j
                    last = j == 5
                    w = tk + [self.pfree.get(("S", sl))] if j == j0 else []
                    t = P.op("pe", lambda e, j=j, kb=kb, qs=qs, kh=kh, psS=psS: e.matmul(
                        psS[:, j * 128:(j + 1) * 128], kh[:, kb * 128:(kb + 1) * 128], qs, start=True, stop=True),
                        waits=w, cnt=self.c_pe if last else None)
                    if last:
                        t_s = t
                if lb == NBL - 1:
                    self.kh_free[h % 2] = t_s
                    self.qh_free[h % 2] = t_s
                Sb, Pb = self.S[sl], self.Pb[sl]
                t_b = P.op("dve", lambda e, j0=j0, psS=psS, Sb=Sb, bt=bt, lb=lb: e.scalar_tensor_tensor(
                    out=Sb[:, j0 * 128:768], in0=psS[:, j0 * 128:768], scalar=scale, in1=bt[:, lb % 2, j0 * 128:768],
                    op0=ALU.mult, op1=ALU.add), waits=[t_s, tb, s_free[sl]], cnt=self.c_dve)
                self.pfree[("S", sl)] = t_b
                if lb == NBL - 1:
                    bt_free[h % 2] = t_b
                t_e = P.op("act", lambda e, j0=j0, Sb=Sb, Pb=Pb: e.activation(out=Pb[:, j0 * 128:768], in_=Sb[:, j0 * 128:768], func=AF.Exp),
                           waits=[t_b, p_free[sl]], cnt=self.c_act)
                s_free[sl] = t_e
                T[s] = dict(h=h, lb=lb, j0=j0, t_e=t_e)
            i = s - 1
            if 0 <= i < N:
                d = T[i]
                h, lb, j0, t_e = d["h"], d["lb"], d["j0"], d["t_e"]
                sl = i % 2
                Pb = self.Pb[sl]
                vh = self.vh[h % 2]
                tv = vv[h]
                t_d = None
                for j in range(j0, 6):
                    last = j == 5
                    t = P.op("pe", lambda e, j=j, Pb=Pb, sl=sl, j0=j0: e.matmul(
                        self.bankB(sl, 128), self.ones_b[:], Pb[:, j * 128:(j + 1) * 128], start=(j == j0), stop=(j == 5)),
                        waits=[t_e, self.pfree.get(("B", sl))] if j == j0 else [], cnt=self.c_pe if last else None)
                    if last:
                        t_d = t
                t_o = None
                for j in range(j0, 6):
                    kb = 2 * lb - 4 + j
                    pos = own(kb) * NBL + kb // 2
                    last = j == 5
                    t = P.op("pe", lambda e, j=j, pos=pos, Pb=Pb, sl=sl, vh=vh, j0=j0: e.matmul(
                        self.bankB(2 + sl, 128), vh[:, pos, :], Pb[:, j * 128:(j + 1) * 128], start=(j == j0), stop=(j == 5)),
                        waits=tv + [self.pfree.get(("B", 2 + sl))] if j == j0 else [], cnt=self.c_pe if last else None)
                    if last:
                        t_o = t
                p_free[sl] = t_o
                if lb == NBL - 1:
                    self.vh_free[h % 2] = t_o
                rd = self.rden[sl]
                t_ln = P.op("act", lambda e, rd=rd, sl=sl: e.activation(out=rd[:], in_=self.bankB(sl, 128), func=AF.Ln),
                            waits=[t_d, rd_free[sl]], cnt=self.c_act)
                self.pfree[("B", sl)] = t_ln
                t_r = P.op("act", lambda e, rd=rd: e.activation(out=rd[:], in_=rd[:], func=AF.Exp, scale=-1.0), cnt=self.c_act)
                t_st = P.op("dve", lambda e, rd=rd, sl=sl, h=h, lb=lb: e.tensor_tensor(
                    out=self.xn[:, h, lb * 128:(lb + 1) * 128], in0=self.bankB(2 + sl, 128), in1=rd[:], op=ALU.mult),
                    waits=[t_o, t_r], cnt=self.c_dve)
                self.pfree[("B", 2 + sl)] = t_st
                rd_free[sl] = t_st
                o_tok.append(t_st)
        self.attn_done = [o_tok[-1], s_free[0], s_free[1]]
        self.qT_free = []
        self.xn_ready = [o_tok[-1]]
        self.xn_ready_k = None

    def oproj(self, W):
        c, P = self.c, self.P
        TT = c.TT
        h_new = {}

        def evac(oc, tt, bank, tok):
            t = P.op("dve", lambda e: e.tensor_tensor(out=self.hT[:, oc, tt * TT:(tt + 1) * TT], in0=bank,
                                                      in1=self.hT[:, oc, tt * TT:(tt + 1) * TT], op=ALU.add),
                     waits=[tok], cnt=self.c_dve)
            h_new[(oc, tt)] = t
            return t

        self.proj_fm(W, evac, extra_first=self.attn_done)
        self.h_tok = list(h_new.values())
        self.h_tok_k = {oc: [h_new[(oc, tt)] for tt in range(self.c.NTT)] for oc in range(self.c.KC)}


def host_prep(cfg, inputs):
    c = cfg
    f32 = np.float32
    x = np.asarray(inputs["x"], f32)
    gvecs = []
    g_ffn = np.asarray(inputs["g_ffn"], f32)
    g_mix = np.asarray(inputs["g_mix"], f32)
    for l in range(c.DEPTH):
        for j in range(2):
            gvecs.append(g_ffn[l, j])
    for l in range(c.DEPTH):
        gvecs.append(g_mix[l])
    gvecs.append(np.asarray(inputs["g_kv"], f32))
    gvecs.append(np.asarray(inputs["g_final"], f32))
    gv = np.stack([g.reshape(c.KC, 128).T for g in gvecs], axis=1).reshape(128, c.NV * c.KC)
    gv = np.ascontiguousarray(gv, f32)
    cst = np.zeros((128, 512), f32)
    cst[:, 0:128] = 1.0
    cst[:, 256:384] = np.eye(128, dtype=f32)
    tq = np.arange(128)[:, None]
    ts = np.arange(128)[None, :]
    tri = (ts < tq).astype(f32)
    ones = np.ones((128, 128), f32)
    zeros = np.zeros((128, 128), f32)
    mask_even = np.concatenate([tri, zeros], axis=1)
    mask_odd = np.concatenate([ones, tri], axis=1)
    rel_bias = np.asarray(inputs["rel_bias_b"], f32)
    kk = np.arange(128)[:, None]
    qq = np.arange(128)[None, :]
    bt_par = []
    for par in range(2):
        tiles_idx = []
        tiles_valid = []
        for j in range(6):
            off = par + 4 - j
            rel = off * 128 + qq - kk
            idx = np.clip(rel, -REL_CLIP, REL_CLIP) + REL_CLIP
            dchunk = -2 * off + (kk >= 64).astype(np.int64) - (qq >= 64).astype(np.int64)
            valid = (dchunk >= -LEFT_CHUNKS) & (dchunk <= 0) & (off >= 0) & (off <= 4)
            tiles_idx.append(idx)
            tiles_valid.append(valid)
        bt_par.append((np.stack(tiles_idx), np.stack(tiles_valid)))
    in_maps = []
    shared = {k: np.ascontiguousarray(np.asarray(inputs[k], f32)) for k in
              ("w_ffn_gate", "w_ffn_up", "w_ffn_down", "w_qkv_a", "w_o_a", "w_kv_shared", "w_q_b", "w_o_b")}
    bt_rank = []
    for r in range(2):
        per_l = []
        for lbi in range(c.NBD):
            rb = rel_bias[lbi]
            arr = np.empty((c.H, 128, 2, 6, 128), f32)
            for pi in range(2):
                idx, valid = bt_par[(pi + r) % 2]
                g = rb[:, idx]
                g = np.where(valid[None], g, f32(NEG))
                arr[:, :, pi, :, :] = np.transpose(g, (0, 2, 1, 3))
            per_l.append(arr.reshape(c.H, 128, 2 * 6 * 128))
        bt_rank.append(np.ascontiguousarray(np.stack(per_l)))
    m2_rank = []
    mb_rank = []
    for r in range(2):
        mm = [mask_even, mask_odd] if r == 0 else [mask_odd, mask_even]
        m2_rank.append(np.ascontiguousarray(np.concatenate(mm, axis=1)))
        mb_rank.append(np.ascontiguousarray(np.where(m2_rank[-1] > 0.5, f32(0.0), f32(-30000.0)).astype(f32)))
    toks = []
    for r in range(2):
        idx = np.concatenate([np.arange(gblock(r, lb) * 128, gblock(r, lb) * 128 + 128) for lb in range(c.NBL)])
        toks.append(idx)
    for core in range(2 * c.BATCH):
        b, r = core // 2, core % 2
        m = dict(shared)
        m["xT"] = np.ascontiguousarray(x[b][toks[r], :].T)
        m["gv"] = gv
        m["cst"] = cst
        m["m2"] = m2_rank[r]
        m["mb"] = mb_rank[r]
        m["bt"] = bt_rank[r]
        in_maps.append(m)
    return in_maps, toks


def host_post(cfg, results, toks):
    c = cfg
    out = np.empty((c.BATCH, c.SEQ, c.D), np.float32)
    for core in range(2 * c.BATCH):
        b, r = core // 2, core % 2
        out[b, toks[r], :] = np.asarray(results[core]["outT"]).T
    return out


_CACHE = {}


def kernel(**inputs):
    cfg = Cfg()
    if "nc" not in _CACHE:
        _CACHE["nc"] = Builder(cfg).build()
    nc = _CACHE["nc"]
    in_maps, toks = host_prep(cfg, inputs)
    res = run_bass_kernel_spmd(nc, in_maps, core_ids=list(range(8)))
    return host_post(cfg, res.results, toks)
```
